# Optimizing a Trainium2 kernel written in Bass

```python
import jax, jax.numpy as jnp
from jax import lax
import numpy as np

D_MODEL = 1024
BATCH = 8
SEQ = 4096
DEPTH = 4

N_MEM = 256
D_FF = 2816
D_CONV = D_MODEL // 2
CONV_WIDTH = 31
D_SGU = D_MODEL - D_CONV
SGU_HEADS = 4
SGU_HEAD_DIM = D_SGU // SGU_HEADS
CHUNK = 128
POOL_WINDOWS = (2, 4, 8, 16)
POOL_GROUPS = len(POOL_WINDOWS)
POOL_GROUP_DIM = D_MODEL // POOL_GROUPS
XA_HEADS = 4
XA_HEAD_DIM = D_MODEL // XA_HEADS
N_EVEN = (DEPTH + 1) // 2
N_ODD = DEPTH // 2
EPS = 1e-6

kernel_name = "hybrid_conv_sgu_pool_macaron_xattn"


def rms_norm(x, g):
    xf = x.astype(jnp.float32)
    y = xf * lax.rsqrt(jnp.mean(xf * xf, axis=-1, keepdims=True) + EPS)
    return (y * g.astype(jnp.float32)).astype(x.dtype)


def layer_norm(x, g, b):
    xf = x.astype(jnp.float32)
    mu = jnp.mean(xf, axis=-1, keepdims=True)
    var = jnp.mean(jnp.square(xf - mu), axis=-1, keepdims=True)
    y = (xf - mu) * lax.rsqrt(var + EPS)
    return (y * g.astype(jnp.float32) + b.astype(jnp.float32)).astype(x.dtype)


def swiglu_ffn(h, w_gu, w_down):
    gu = h @ w_gu
    gate, up = jnp.split(gu, 2, axis=-1)
    return (jax.nn.silu(gate) * up) @ w_down


def conformer_conv(a_val, a_gate, conv_w, conv_b, ln_g, ln_b):
    z = a_val * jax.nn.sigmoid(a_gate)
    rhs = conv_w[:, None, :].astype(z.dtype)
    z = lax.conv_general_dilated(
        z, rhs, window_strides=(1,), padding=[(CONV_WIDTH - 1, 0)],
        dimension_numbers=("NWC", "WIO", "NWC"), feature_group_count=z.shape[-1])
    z = z + conv_b
    z = layer_norm(z, ln_g, ln_b)
    return jax.nn.silu(z)


def chunked_causal_sgu(u, v, ln_g, ln_b, w_s, b_s):
    bsz, seq, _ = u.shape
    n_chunk = seq // CHUNK
    v = layer_norm(v, ln_g, ln_b)
    v = v.reshape(bsz, n_chunk, CHUNK, SGU_HEADS, SGU_HEAD_DIM)
    mask = jnp.tril(jnp.ones((CHUNK, CHUNK), dtype=w_s.dtype))
    w = w_s * mask[None]
    mixed = jnp.einsum("hts,bnshc->bnthc", w, v) + b_s.T[:, :, None]
    out = u.reshape(bsz, n_chunk, CHUNK, SGU_HEADS, SGU_HEAD_DIM) * mixed
    return out.reshape(bsz, seq, D_SGU)


def even_mixer(h, w_in, conv_w, conv_b, conv_ln_g, conv_ln_b,
               sgu_ln_g, sgu_ln_b, sgu_w, sgu_b, w_out):
    p = h @ w_in
    a_val = p[..., :D_CONV]
    a_gate = p[..., D_CONV:2 * D_CONV]
    zb = jax.nn.gelu(p[..., 2 * D_CONV:], approximate=False)
    b_u = zb[..., :D_SGU]
    b_v = zb[..., D_SGU:]
    ya = conformer_conv(a_val, a_gate, conv_w, conv_b, conv_ln_g, conv_ln_b)
    yb = chunked_causal_sgu(b_u, b_v, sgu_ln_g, sgu_ln_b, sgu_w, sgu_b)
    return jnp.concatenate([ya, yb], axis=-1) @ w_out


def odd_mixer(h, w_in, w_group, scale, w_out):
    p = h @ w_in
    bsz, seq, _ = p.shape
    pf = p.astype(jnp.float32)
    csum = jnp.cumsum(pf, axis=1)
    c_pad = jnp.concatenate([jnp.zeros((bsz, 1, D_MODEL), jnp.float32), csum], axis=1)
    pos = jnp.arange(1, seq + 1, dtype=jnp.int32)
    pooled = []
    for g, w in enumerate(POOL_WINDOWS):
        sl = slice(g * POOL_GROUP_DIM, (g + 1) * POOL_GROUP_DIM)
        upper = c_pad[:, 1:, sl]
        lower = jnp.concatenate(
            [jnp.zeros((bsz, w - 1, POOL_GROUP_DIM), jnp.float32), c_pad[:, :seq + 1 - w, sl]], axis=1)
        count = jnp.minimum(pos, w).astype(jnp.float32)[None, :, None]
        pooled.append((upper - lower) / count)
    d = (jnp.concatenate(pooled, axis=-1) - pf).astype(p.dtype)
    d = d.reshape(bsz, seq, POOL_GROUPS, POOL_GROUP_DIM)
    d = jnp.einsum("bsgc,gcd->bsgd", d, w_group).reshape(bsz, seq, D_MODEL)
    return (d * scale) @ w_out


def memory_cross_attention(h, mem, mem_g, w_q, w_kv, w_o):
    bsz, seq, _ = h.shape
    q = (h @ w_q).reshape(bsz, seq, XA_HEADS, XA_HEAD_DIM)
    kv = rms_norm(mem, mem_g) @ w_kv
    k, v = jnp.split(kv, 2, axis=-1)
    k = k.reshape(bsz, -1, XA_HEADS, XA_HEAD_DIM)
    v = v.reshape(bsz, -1, XA_HEADS, XA_HEAD_DIM)
    s = jnp.einsum("bshd,bmhd->bhsm", q, k).astype(jnp.float32) * (XA_HEAD_DIM ** -0.5)
    a = jax.nn.softmax(s, axis=-1).astype(v.dtype)
    o = jnp.einsum("bhsm,bmhd->bshd", a, v).reshape(bsz, seq, D_MODEL)
    return o @ w_o


def setup_inputs(seed: int = 0) -> dict:
    key = jax.random.key(seed)
    ks = jax.random.split(key, 32)
    f32 = jnp.float32

    def nrm(k, shape, fan_in):
        return jax.random.normal(k, shape, f32) * (fan_in ** -0.5)

    def gain(k, shape):
        return 1.0 + 0.05 * jax.random.normal(k, shape, f32)

    def small(k, shape):
        return 0.02 * jax.random.normal(k, shape, f32)

    return {
        "x": jax.random.normal(ks[0], (BATCH, SEQ, D_MODEL), f32),
        "mem": jax.random.normal(ks[1], (BATCH, N_MEM, D_MODEL), f32),
        "ffn1_pre_g": gain(ks[2], (DEPTH, D_MODEL)),
        "ffn1_w_gu": nrm(ks[3], (DEPTH, D_MODEL, 2 * D_FF), D_MODEL),
        "ffn1_w_down": nrm(ks[4], (DEPTH, D_FF, D_MODEL), D_FF),
        "ffn1_post_g": gain(ks[5], (DEPTH, D_MODEL)),
        "mix_pre_g": gain(ks[6], (DEPTH, D_MODEL)),
        "mix_post_g": gain(ks[7], (DEPTH, D_MODEL)),
        "ev_w_in": nrm(ks[8], (N_EVEN, D_MODEL, 2 * D_CONV + 2 * D_SGU), D_MODEL),
        "ev_conv_w": nrm(ks[9], (N_EVEN, CONV_WIDTH, D_CONV), CONV_WIDTH),
        "ev_conv_b": small(ks[10], (N_EVEN, D_CONV)),
        "ev_conv_ln_g": gain(ks[11], (N_EVEN, D_CONV)),
        "ev_conv_ln_b": small(ks[12], (N_EVEN, D_CONV)),
        "ev_sgu_ln_g": gain(ks[13], (N_EVEN, D_SGU)),
        "ev_sgu_ln_b": small(ks[14], (N_EVEN, D_SGU)),
        "ev_sgu_w": nrm(ks[15], (N_EVEN, SGU_HEADS, CHUNK, CHUNK), CHUNK),
        "ev_sgu_b": gain(ks[16], (N_EVEN, SGU_HEADS, CHUNK)),
        "ev_w_out": nrm(ks[17], (N_EVEN, D_CONV + D_SGU, D_MODEL), D_CONV + D_SGU),
        "od_w_in": nrm(ks[18], (N_ODD, D_MODEL, D_MODEL), D_MODEL),
        "od_w_group": nrm(ks[19], (N_ODD, POOL_GROUPS, POOL_GROUP_DIM, POOL_GROUP_DIM), POOL_GROUP_DIM),
        "od_scale": gain(ks[20], (N_ODD, D_MODEL)),
        "od_w_out": nrm(ks[21], (N_ODD, D_MODEL, D_MODEL), D_MODEL),
        "xa_pre_g": gain(ks[22], (DEPTH, D_MODEL)),
        "xa_mem_g": gain(ks[23], (DEPTH, D_MODEL)),
        "xa_w_q": nrm(ks[24], (DEPTH, D_MODEL, D_MODEL), D_MODEL),
        "xa_w_kv": nrm(ks[25], (DEPTH, D_MODEL, 2 * D_MODEL), D_MODEL),
        "xa_w_o": nrm(ks[26], (DEPTH, D_MODEL, D_MODEL), D_MODEL),
        "xa_post_g": gain(ks[27], (DEPTH, D_MODEL)),
        "ffn2_pre_g": gain(ks[28], (DEPTH, D_MODEL)),
        "ffn2_w_gu": nrm(ks[29], (DEPTH, D_MODEL, 2 * D_FF), D_MODEL),
        "ffn2_w_down": nrm(ks[30], (DEPTH, D_FF, D_MODEL), D_FF),
        "ffn2_post_g": gain(ks[31], (DEPTH, D_MODEL)),
    }


def reference(x, mem, ffn1_pre_g, ffn1_w_gu, ffn1_w_down, ffn1_post_g,
              mix_pre_g, mix_post_g,
              ev_w_in, ev_conv_w, ev_conv_b, ev_conv_ln_g, ev_conv_ln_b,
              ev_sgu_ln_g, ev_sgu_ln_b, ev_sgu_w, ev_sgu_b, ev_w_out,
              od_w_in, od_w_group, od_scale, od_w_out,
              xa_pre_g, xa_mem_g, xa_w_q, xa_w_kv, xa_w_o, xa_post_g,
              ffn2_pre_g, ffn2_w_gu, ffn2_w_down, ffn2_post_g):
    h = x
    for i in range(DEPTH):
        f = swiglu_ffn(rms_norm(h, ffn1_pre_g[i]), ffn1_w_gu[i], ffn1_w_down[i])
        h = h + 0.5 * rms_norm(f, ffn1_post_g[i])
        hn = rms_norm(h, mix_pre_g[i])
        if i % 2 == 0:
            e = i // 2
            m = even_mixer(hn, ev_w_in[e], ev_conv_w[e], ev_conv_b[e],
                           ev_conv_ln_g[e], ev_conv_ln_b[e], ev_sgu_ln_g[e],
                           ev_sgu_ln_b[e], ev_sgu_w[e], ev_sgu_b[e], ev_w_out[e])
        else:
            o = i // 2
            m = odd_mixer(hn, od_w_in[o], od_w_group[o], od_scale[o], od_w_out[o])
        h = h + rms_norm(m, mix_post_g[i])
        c = memory_cross_attention(rms_norm(h, xa_pre_g[i]), mem, xa_mem_g[i],
                                   xa_w_q[i], xa_w_kv[i], xa_w_o[i])
        h = h + rms_norm(c, xa_post_g[i])
        f = swiglu_ffn(rms_norm(h, ffn2_pre_g[i]), ffn2_w_gu[i], ffn2_w_down[i])
        h = h + 0.5 * rms_norm(f, ffn2_post_g[i])
    return h
```

```python
import contextlib
import numpy as np
import concourse.bass as bass
import concourse.mybir as mybir
from concourse.bass_utils import run_bass_kernel_spmd

F32 = mybir.dt.float32
BF16 = mybir.dt.bfloat16
AF = mybir.ActivationFunctionType
ALU = mybir.AluOpType

PE, ACT, DVE, POOL, SP = "pe", "act", "dve", "pool", "sp"
ENGS = (PE, ACT, DVE, POOL, SP)

D = 1024
DFF = 2816
SEQ = 4096
TT = 512
NMEM = 256
NSLOT = 5
SLOT = 2816
EPS = 1e-6
CW = 31
NRS = 2


class Sched:
    def __init__(self, strict_same=(ACT, DVE, POOL)):
        self.q = {e: [] for e in ENGS}
        self.last_w = {}
        self.readers = {}
        self.waited = {e: {} for e in ENGS}
        self.dma_cnt = {}
        self.strict_same = set(strict_same)

    def _need(self, eng, tok, waits):
        if tok is None:
            return
        if tok[0] == "e":
            _, peng, idx = tok
            if peng == eng and eng not in self.strict_same:
                return
            key = ("e", peng)
            if self.waited[eng].get(key, -1) >= idx:
                return
            self.waited[eng][key] = idx
            self.q[peng][idx]["inc"] = True
            waits.append(tok)
        else:
            _, sk, val = tok
            key = ("d", sk)
            if self.waited[eng].get(key, -1) >= val:
                return
            self.waited[eng][key] = val
            waits.append(tok)

    def _deps(self, eng, reads, writes):
        waits = []
        for r in reads:
            self._need(eng, self.last_w.get(r), waits)
        for w in writes:
            self._need(eng, self.last_w.get(w), waits)
            for t in self.readers.get(w, ()):
                self._need(eng, t, waits)
        return waits

    def _post(self, tok, reads, writes):
        for r in reads:
            self.readers.setdefault(r, []).append(tok)
        for w in writes:
            self.last_w[w] = tok
            self.readers[w] = []

    def op(self, eng, fn, reads=(), writes=()):
        waits = self._deps(eng, reads, writes)
        idx = len(self.q[eng])
        self.q[eng].append({"fn": fn, "waits": waits, "inc": False, "dma": None})
        tok = ("e", eng, idx)
        self._post(tok, reads, writes)
        return tok

    def dma(self, eng, fn, semkey, reads=(), writes=()):
        waits = self._deps(eng, reads, writes)
        n = self.dma_cnt.get(semkey, 0) + 1
        self.dma_cnt[semkey] = n
        self.q[eng].append({"fn": fn, "waits": waits, "inc": False, "dma": semkey})
        tok = ("d", semkey, 16 * n)
        self._post(tok, reads, writes)
        return tok

    def final_wait(self, eng, toks):
        waits = []
        for t in toks:
            self._need(eng, t, waits)
        self.q[eng].append({"fn": None, "waits": waits, "inc": False, "dma": None})

    def emit(self, nc, stack):
        esem = {e: stack.enter_context(nc.semaphore("s_" + e)) for e in ENGS}
        dsem = {k: stack.enter_context(nc.semaphore("d_%s" % str(k))) for k in self.dma_cnt}
        cum = {}
        for e in ENGS:
            c = 0
            arr = []
            for o in self.q[e]:
                if o["inc"]:
                    c += 1
                arr.append(c)
            cum[e] = arr
        block = stack.enter_context(nc.Block())

        def run(e, engobj):
            for o in self.q[e]:
                for t in o["waits"]:
                    if t[0] == "e":
                        engobj.wait_ge(esem[t[1]], cum[t[1]][t[2]])
                    else:
                        engobj.wait_ge(dsem[t[1]], t[2])
                if o["fn"] is None:
                    continue
                ins = o["fn"](engobj)
                if o["dma"] is not None:
                    ins.then_inc(dsem[o["dma"]], 16)
                elif o["inc"]:
                    ins.then_inc(esem[e], 1)

        @block.tensor
        def _(eng):
            run(PE, eng)

        @block.scalar
        def _(eng):
            run(ACT, eng)

        @block.vector
        def _(eng):
            run(DVE, eng)

        @block.gpsimd
        def _(eng):
            run(POOL, eng)

        @block.sync
        def _(eng):
            run(SP, eng)


def vec_layout():
    cols = {}
    n = 0
    for l in range(4):
        for nm in ("f1pre", "f1post", "mpre", "mpost", "xpre", "xmem", "xpost", "f2pre", "f2post"):
            cols[(nm, l)] = n
            n += 8
    for e in range(2):
        for nm in ("cb", "clg", "clb", "slg", "slb"):
            cols[(nm, e)] = n
            n += 4
        cols[("cw", e)] = n
        n += 4 * CW
    for o in range(2):
        cols[("osc", o)] = n
        n += 8
    return cols, n


VCOLS, NVEC = vec_layout()
C_MASK, C_ID, C_INV, NCST = 0, 128, 256, 320


class WStream:
    def __init__(self, plan=None):
        self.plan = plan
        self.rec = []
        self.i = 0
        self.issued = 0
        self.seen = set()

    def next(self, S, B, name, idx, E):
        i = self.i
        self.i += 1
        if self.plan is None:
            self.rec.append((name, idx, E))
            return i % NSLOT, "ws%d" % (i % NSLOT)
        WB = B.WB
        upto = min(i + NSLOT - 1, len(self.plan) - 1)
        while self.issued <= upto:
            u = self.issued
            nm, ix, uE = self.plan[u]
            sl = u % NSLOT
            key = (nm, ix)
            if key not in self.seen:
                self.seen.add(key)
                src = B.d[nm][ix]
                S.dma(POOL, lambda e, sl=sl, src=src, uE=uE: e.dma_start(out=WB[:, sl, 0:uE], in_=src[:, 0:uE]),
                      "w%d" % sl, writes=["ws%d" % sl])
                if nm in B.scr:
                    dst = B.scr[nm][ix]
                    S.dma(SP, lambda e, sl=sl, dst=dst, uE=uE: e.dma_start(out=dst[:, 0:uE], in_=WB[:, sl, 0:uE]),
                          "ss%d" % sl, reads=["ws%d" % sl], writes=[("scr", nm, ix)])
            else:
                src = B.scr[nm][ix]
                S.dma(POOL, lambda e, sl=sl, src=src, uE=uE: e.dma_start(out=WB[:, sl, 0:uE], in_=src[:, 0:uE]),
                      "w%d" % sl, reads=[("scr", nm, ix)], writes=["ws%d" % sl])
            self.issued += 1
        return i % NSLOT, "ws%d" % (i % NSLOT)


class Builder:
    def __init__(self, nc, ntiles, layers, debug=False):
        self.nc = nc
        self.ntiles = ntiles
        self.layers = layers
        dt = nc.dram_tensor
        self.d = {}
        for name, shape in self.dram_specs(ntiles).items():
            self.d[name] = dt(name, list(shape), F32, kind="ExternalInput").ap()
        self.yT = dt("yT", [D, ntiles * TT], F32, kind="ExternalOutput").ap()
        self.scr = {}
        for name, shape in self.dram_specs(ntiles).items():
            if name in ("wgu1", "wgu2", "wd1", "wd2", "evin", "evcv", "evout", "odin", "odg", "odout", "wq", "wo"):
                self.scr[name] = dt(name + "_b", list(shape), BF16).ap()

    @staticmethod
    def dram_specs(ntiles):
        return {
            "xT": (D, ntiles * TT), "memT": (D, NMEM),
            "wgu1": (4, 22, 128, 2048), "wgu2": (4, 22, 128, 2048),
            "wd1": (4, 8, 128, 2816), "wd2": (4, 8, 128, 2816),
            "evin": (2, 8, 128, 2048), "evcv": (2, 8, 128, 2048), "evout": (2, 4, 128, 2048),
            "odin": (2, 4, 128, 2048), "odg": (2, 128, 2048), "odout": (2, 4, 128, 2048),
            "wq": (4, 4, 128, 2048), "wo": (4, 4, 128, 2048), "wk": (4, 4, 128, 2048), "wv": (4, 4, 128, 2048),
            "vec": (128, NVEC), "cst": (128, NCST), "wst": (2, 128, 512), "bsb": (2, 128, 512),
        }

    def alloc(self, st):
        nc = self.nc
        sb = lambda n, s, t: st.enter_context(nc.sbuf_tensor(n, s, t))
        self.H = [sb("H0", [128, 8, TT], F32), sb("H1", [128, 8, TT], F32)]
        self.XN = [sb("XN0", [128, 8, TT], BF16), sb("XN1", [128, 8, TT], BF16)]
        self.HID = sb("HID", [128, 22, TT], BF16)
        self.F = sb("F", [128, 8, TT], F32)
        self.SQ = sb("SQ", [128, 8, TT], BF16)
        self.RS = [sb("RS%d" % i, [128, TT], F32) for i in range(NRS)]
        self.RD = [sb("RD%d" % i, [128, TT], F32) for i in range(4)]
        self.SG = [sb("SG%d" % i, [128, TT], F32) for i in range(2)]
        self.WB = sb("WB", [128, NSLOT, SLOT], BF16)
        self.KT = sb("KT", [128, 4, 8, NMEM], BF16)
        self.V = sb("V", [128, 4, 2, D], BF16)
        self.SCR = sb("SCR", [128, 8, 544], F32)
        self.E = sb("E", [128, 2, 2, TT], BF16)
        self.CARZ = sb("CARZ", [128, 2, 4, 30], BF16)
        self.ZB = sb("ZB", [128, 4, 544], BF16)
        self.CARP = sb("CARP", [128, 2, 8, 16], F32)
        self.VEC = sb("VEC", [128, NVEC], F32)
        self.CST = sb("CST", [128, NCST], F32)
        self.ONESB = sb("ONESB", [128, 128], BF16)
        self.ONESF = sb("ONESF", [128, 128], F32)
        self.IDB = sb("IDB", [128, 128], BF16)
        self.WST = sb("WST", [128, 2, 512], BF16)
        self.BSB = sb("BSB", [128, 2, 512], F32)
        self.PS = [st.enter_context(nc.psum_tensor("PS%d" % i, [128, TT], F32)) for i in range(7)]
        self.TPB = st.enter_context(nc.psum_tensor("TPB", [128, 1024], BF16))
        self.pi = 0
        self.sti = 0
        self.rsi = 0
        self.sgi = 0

    def bank(self):
        i = self.pi % 6
        self.pi += 1
        return self.PS[i], "PS%d" % i

    def stbank(self):
        return self.bank()

    def chainbank(self, s):
        return self.PS[6], "PS6"

    def rs(self):
        i = self.rsi % NRS
        self.rsi += 1
        return self.RS[i], "RS%d" % i

    def sg(self):
        i = self.sgi % 2
        self.sgi += 1
        return self.SG[i], "SG%d" % i

    def vcol(self, nm, idx, c):
        o = VCOLS[(nm, idx)] + c
        return self.VEC[:, o:o + 1]

    def emit_all(self, S, WS):
        self.S, self.WS = S, WS
        self.pi = self.sti = self.rsi = self.sgi = 0
        self.pending = []
        self.postR = {}
        self.preR = {}
        d = self.d
        S.dma(SP, lambda e: e.dma_start(out=self.VEC[:], in_=d["vec"]), "par", writes=["VEC"])
        S.dma(SP, lambda e: e.dma_start(out=self.CST[:], in_=d["cst"]), "par2", writes=["CST"])
        S.dma(SP, lambda e: e.dma_start(out=self.BSB[:], in_=d["bsb"].rearrange("e p f -> p e f")), "par3", writes=["BSB"])
        S.op(DVE, lambda e: e.memset(self.ONESB[:], 1.0), writes=["ONESB"])
        S.op(DVE, lambda e: e.memset(self.ONESF[:], 1.0), writes=["ONESF"])
        S.op(DVE, lambda e: e.memset(self.CARZ[:], 0.0), writes=["CARZ0", "CARZ1"])
        S.op(DVE, lambda e: e.memset(self.CARP[:], 0.0), writes=["CARP0", "CARP1"])
        S.op(DVE, lambda e: e.tensor_copy(out=self.IDB[:], in_=self.CST[:, C_ID:C_ID + 128]), reads=["CST"], writes=["IDB"])
        for l in range(4):
            for nm in ("f1post", "f2post"):
                o = VCOLS[(nm, l)]
                S.op(DVE, lambda e, o=o: e.tensor_scalar(out=self.VEC[:, o:o + 8], in0=self.VEC[:, o:o + 8], scalar1=0.5, scalar2=None, op0=ALU.mult),
                     reads=["VEC"], writes=["VEC"])
        for ev in range(2):
            S.dma(SP, lambda e, ev=ev: e.dma_start(out=self.SG[0][:], in_=d["wst"][ev]), "par4", writes=["SG0"])
            for h in range(4):
                S.op(DVE, lambda e, ev=ev, h=h: e.tensor_tensor(out=self.WST[:, ev, h * 128:(h + 1) * 128], in0=self.SG[0][:, h * 128:(h + 1) * 128],
                                                                in1=self.CST[:, C_MASK:C_MASK + 128], op=ALU.mult),
                     reads=["SG0", "CST"], writes=["WST"])
        for l in self.layers:
            self.kv_prologue(l)
        subl = [(kind, l) for l in self.layers for kind in ("f1", "mix", "xa", "f2")]
        prename = {"f1": "f1pre", "mix": "mpre", "xa": "xpre", "f2": "f2pre"}
        self.pending = []
        self.out_toks = []
        nstream = 2 if self.ntiles >= 2 else 1
        for tp in range(self.ntiles // nstream):
            for s in range(nstream):
                t = tp * nstream + s
                S.dma(SP, lambda e, t=t, s=s: e.dma_start(out=self.H[s][:], in_=d["xT"][:, t * TT:(t + 1) * TT].rearrange("(c p) t -> p c t", p=128)),
                      "hin%d" % s, writes=["H%d_%d" % (s, c) for c in range(8)])
            for s in range(nstream):
                self.pre_norm(s, prename[subl[0][0]], subl[0][1])
            for k, (kind, l) in enumerate(subl):
                for s in range(nstream):
                    t = tp * nstream + s
                    self.flush(s)
                    if kind == "f1":
                        self.ffn(s, l, 1)
                    elif kind == "f2":
                        self.ffn(s, l, 2)
                    elif kind == "xa":
                        self.xattn(s, l)
                    elif l % 2 == 0:
                        self.mixer_even(s, l, t)
                    else:
                        self.mixer_odd(s, l, t)
                    if k + 1 < len(subl):
                        nk, nl = subl[k + 1]
                        self.defer(s, 7, lambda s=s: self.pre_norm_a(s))
                        self.defer(s, 9, lambda s=s: self.pre_norm_b(s))
                        self.defer(s, 11, lambda s=s, nk=nk, nl=nl: self.pre_norm_c(s, prename[nk], nl))
                    else:
                        self.defer(s, 7, lambda s=s, t=t: self.store(s, t))
            self.flush(None)
        S.final_wait(SP, self.out_toks[-2:])

    def store(self, s, t):
        tok = self.S.dma(SP, lambda e: e.dma_start(out=self.yT[:, t * TT:(t + 1) * TT].rearrange("(c p) t -> p c t", p=128), in_=self.H[s][:]),
                         "hout%d" % s, reads=["H%d_%d" % (s, c) for c in range(8)])
        self.out_toks.append(tok)

    def defer(self, s, cd, fn, tag=""):
        self.pending.append([cd, s, fn, tag])

    def flush_post(self):
        last = -1
        for i, it in enumerate(self.pending):
            if it[3] == "post":
                last = i
        for _ in range(last + 1):
            self.pending.pop(0)[2]()

    def tick(self):
        for it in self.pending:
            it[0] -= 1
        while self.pending and self.pending[0][0] <= 0:
            self.pending.pop(0)[2]()

    def flush(self, s):
        last = -1
        for i, it in enumerate(self.pending):
            if s is None or it[1] == s:
                last = i
        for _ in range(last + 1):
            self.pending.pop(0)[2]()

    def rstd_from(self, st, stn, scale, eps, ded=None):
        S = self.S
        if ded is None:
            R, rn = self.rs()
        else:
            R, rn = self.RD[ded], "RD%d" % ded
        S.op(ACT, lambda e: e.activation(out=R[:], in_=st[:], func=AF.Ln, bias=eps, scale=scale), reads=[stn], writes=[rn])
        S.op(ACT, lambda e: e.activation(out=R[:], in_=R[:], func=AF.Exp, scale=-0.5), reads=[rn], writes=[rn])
        return R, rn

    def pre_norm_a(self, s):
        S = self.S
        H = self.H[s]
        for c in range(8):
            if c % 4 != 3:
                S.op(ACT, lambda e, c=c: e.activation(out=self.SQ[:, c, :], in_=H[:, c, :], func=AF.Square),
                     reads=["H%d_%d" % (s, c)], writes=["SQ%d" % c])
            else:
                S.op(POOL, lambda e, c=c: e.tensor_tensor(out=self.SQ[:, c, :], in0=H[:, c, :], in1=H[:, c, :], op=ALU.mult),
                     reads=["H%d_%d" % (s, c)], writes=["SQ%d" % c])

    def pre_norm_b(self, s):
        S = self.S
        st, stn = self.stbank()
        for c in range(8):
            S.op(PE, lambda e, c=c: e.matmul(st[:], lhsT=self.ONESB[:], rhs=self.SQ[:, c, :], start=(c == 0), stop=(c == 7)),
                 reads=["SQ%d" % c, "ONESB"], writes=[stn])
        self.preR[s] = self.rstd_from(st, stn, 1.0 / D, EPS, ded=2 * s + 1)

    def pre_norm_c(self, s, gname, l):
        S = self.S
        H = self.H[s]
        R, rn = self.preR[s]
        for c in range(8):
            S.op(DVE, lambda e, c=c: e.scalar_tensor_tensor(out=self.XN[s][:, c, :], in0=H[:, c, :], scalar=self.vcol(gname, l, c), in1=R[:],
                                                            op0=ALU.mult, op1=ALU.mult),
                 reads=["H%d_%d" % (s, c), rn, "VEC"], writes=["XN%d_%d" % (s, c)])

    def pre_norm(self, s, gname, l):
        self.pre_norm_a(s)
        self.pre_norm_b(s)
        self.pre_norm_c(s, gname, l)

    def out_chunk(self, j, ps, psn, gname, l):
        S = self.S
        if j == 0:
            self.flush(None)
        S.op(ACT, lambda e: e.activation(out=self.F[:, j, :], in_=ps[:], func=AF.Identity, scale=self.vcol(gname, l, j)),
             reads=[psn, "VEC"], writes=["F%d" % j])
        S.op(ACT, lambda e: e.activation(out=self.SQ[:, j, :], in_=ps[:], func=AF.Square), reads=[psn], writes=["SQ%d" % j])

    def out_stat(self, j, st, stn):
        self.S.op(PE, lambda e: e.matmul(st[:], lhsT=self.ONESB[:], rhs=self.SQ[:, j, :], start=(j == 0), stop=(j == 7)),
                  reads=["SQ%d" % j, "ONESB"], writes=[stn])

    def post_a(self, s, prev, st, stn):
        self.out_stat(prev, st, stn)
        self.postR[s] = self.rstd_from(st, stn, 1.0 / D, EPS, ded=2 * s)

    def post_b(self, s, c0, c1):
        S = self.S
        H = self.H[s]
        R, rn = self.postR[s]
        for c in range(c0, c1):
            S.op(POOL, lambda e, c=c: e.tensor_tensor(out=self.F[:, c, :], in0=self.F[:, c, :], in1=R[:], op=ALU.mult),
                 reads=["F%d" % c, rn], writes=["F%d" % c])
            S.op(DVE, lambda e, c=c: e.tensor_tensor(out=H[:, c, :], in0=H[:, c, :], in1=self.F[:, c, :], op=ALU.add),
                 reads=["F%d" % c, "H%d_%d" % (s, c)], writes=["H%d_%d" % (s, c)])

    def defer_post(self, s, prev, st, stn):
        self.defer(s, 1, lambda: self.post_a(s, prev, st, stn), "post")
        self.defer(s, 2, lambda: self.post_b(s, 0, 4), "post")
        self.defer(s, 4, lambda: self.post_b(s, 4, 8), "post")

    def proj_out(self, s, wname, widx, nunits, rhs_list, gname, l):
        S = self.S
        K = len(rhs_list)
        st, stn = self.chainbank(s)
        prev = None
        for u in range(4):
            sl, sn = self.WS.next(S, self, wname, (widx, u), 2 * K * 128)
            for jj in range(2):
                j = 2 * u + jj
                ps, psn = self.bank()
                for k in range(K):
                    rap, rname = rhs_list[k]
                    off = (jj * K + k) * 128
                    S.op(PE, lambda e, ps=ps, sl=sl, off=off, rap=rap, k=k: e.matmul(ps[:], lhsT=self.WB[:, sl, off:off + 128], rhs=rap, start=(k == 0), stop=(k == K - 1)),
                         reads=[sn, rname], writes=[psn])
                self.out_chunk(j, ps, psn, gname, l)
                if prev is not None:
                    self.out_stat(prev, st, stn)
                prev = j
            self.tick()
        self.defer_post(s, prev, st, stn)

    def ffn(self, s, l, which):
        S = self.S
        pre, post = ("f1pre", "f1post") if which == 1 else ("f2pre", "f2post")
        wgu = self.d["wgu%d" % which]
        wd = self.d["wd%d" % which]
        for j in range(22):
            sl, sn = self.WS.next(S, self, "wgu%d" % which, (l, j), 2048)
            g, gn = self.bank()
            u, un = self.bank()
            for k in range(8):
                S.op(PE, lambda e, g=g, sl=sl, k=k: e.matmul(g[:], lhsT=self.WB[:, sl, k * 128:(k + 1) * 128], rhs=self.XN[s][:, k, :], start=(k == 0), stop=(k == 7)),
                     reads=[sn, "XN%d_%d" % (s, k)], writes=[gn])
            for k in range(8):
                S.op(PE, lambda e, u=u, sl=sl, k=k: e.matmul(u[:], lhsT=self.WB[:, sl, 1024 + k * 128:1024 + (k + 1) * 128], rhs=self.XN[s][:, k, :], start=(k == 0), stop=(k == 7)),
                     reads=[sn, "XN%d_%d" % (s, k)], writes=[un])
            sg, sgn = self.sg()
            S.op(ACT, lambda e, g=g, sg=sg: e.activation(out=sg[:], in_=g[:], func=AF.Silu), reads=[gn], writes=[sgn])
            S.op(DVE, lambda e, u=u, sg=sg, j=j: e.tensor_tensor(out=self.HID[:, j, :], in0=sg[:], in1=u[:], op=ALU.mult),
                 reads=[sgn, un], writes=["HID%d" % j])
            self.tick()
        st, stn = self.chainbank(s)
        prev = None
        for j in range(8):
            sl, sn = self.WS.next(S, self, "wd%d" % which, (l, j), 2816)
            ps, psn = self.bank()
            for k in range(22):
                S.op(PE, lambda e, ps=ps, sl=sl, k=k: e.matmul(ps[:], lhsT=self.WB[:, sl, k * 128:(k + 1) * 128], rhs=self.HID[:, k, :], start=(k == 0), stop=(k == 21)),
                     reads=[sn, "HID%d" % k], writes=[psn])
            self.out_chunk(j, ps, psn, post, l)
            if prev is not None:
                self.out_stat(prev, st, stn)
            prev = j
            self.tick()
        self.defer_post(s, prev, st, stn)

    def kv_prologue(self, l):
        S = self.S
        d = self.d
        M = self.SCR
        S.dma(SP, lambda e: e.dma_start(out=M[:, :, 0:NMEM], in_=d["memT"].rearrange("(c p) m -> p c m", p=128)), "memin",
              writes=["SCR%d" % c for c in range(8)])
        st, stn = self.stbank()
        for c in range(8):
            S.op(ACT, lambda e, c=c: e.activation(out=self.SQ[:, c, 0:NMEM], in_=M[:, c, 0:NMEM], func=AF.Square), reads=["SCR%d" % c], writes=["SQ%d" % c])
        for c in range(8):
            S.op(PE, lambda e, c=c: e.matmul(st[:, 0:NMEM], lhsT=self.ONESB[:], rhs=self.SQ[:, c, 0:NMEM], start=(c == 0), stop=(c == 7)),
                 reads=["SQ%d" % c, "ONESB"], writes=[stn])
        R, rn = self.rs()
        S.op(ACT, lambda e: e.activation(out=R[:, 0:NMEM], in_=st[:, 0:NMEM], func=AF.Ln, bias=EPS, scale=1.0 / D), reads=[stn], writes=[rn])
        S.op(ACT, lambda e: e.activation(out=R[:, 0:NMEM], in_=R[:, 0:NMEM], func=AF.Exp, scale=-0.5), reads=[rn], writes=[rn])
        for c in range(8):
            S.op(DVE, lambda e, c=c: e.scalar_tensor_tensor(out=self.HID[:, c, 0:NMEM], in0=M[:, c, 0:NMEM], scalar=self.vcol("xmem", l, c), in1=R[:, 0:NMEM],
                                                            op0=ALU.mult, op1=ALU.mult),
                 reads=["SCR%d" % c, rn, "VEC"], writes=["HID%d" % c])
        for u in range(4):
            sl, sn = self.WS.next(S, self, "wk", (l, u), 2048)
            for jj in range(2):
                j = 2 * u + jj
                ps, psn = self.bank()
                for k in range(8):
                    off = (jj * 8 + k) * 128
                    S.op(PE, lambda e, ps=ps, sl=sl, off=off, k=k: e.matmul(ps[:, 0:NMEM], lhsT=self.WB[:, sl, off:off + 128], rhs=self.HID[:, k, 0:NMEM], start=(k == 0), stop=(k == 7)),
                         reads=[sn, "HID%d" % k], writes=[psn])
                S.op(ACT, lambda e, ps=ps, j=j: e.activation(out=self.KT[:, l, j, :], in_=ps[:, 0:NMEM], func=AF.Identity), reads=[psn], writes=["KT%d" % l])
        for nq in range(4):
            sl, sn = self.WS.next(S, self, "wv", (l, nq), 2048)
            for mc in range(2):
                ps, psn = self.bank()
                for k in range(8):
                    S.op(PE, lambda e, ps=ps, sl=sl, k=k, mc=mc: e.matmul(ps[:, 0:256], lhsT=self.HID[:, k, mc * 128:(mc + 1) * 128], rhs=self.WB[:, sl, k * 256:(k + 1) * 256], start=(k == 0), stop=(k == 7)),
                         reads=[sn, "HID%d" % k], writes=[psn])
                S.op(DVE, lambda e, ps=ps, mc=mc, nq=nq: e.tensor_copy(out=self.V[:, l, mc, nq * 256:(nq + 1) * 256], in_=ps[:, 0:256]), reads=[psn], writes=["V%d" % l])

    def xattn(self, s, l):
        S = self.S
        d = self.d
        Q = self.HID
        for u in range(4):
            sl, sn = self.WS.next(S, self, "wq", (l, u), 2048)
            for jj in range(2):
                j = 2 * u + jj
                ps, psn = self.bank()
                for k in range(8):
                    off = (jj * 8 + k) * 128
                    S.op(PE, lambda e, ps=ps, sl=sl, off=off, k=k: e.matmul(ps[:], lhsT=self.WB[:, sl, off:off + 128], rhs=self.XN[s][:, k, :], start=(k == 0), stop=(k == 7)),
                         reads=[sn, "XN%d_%d" % (s, k)], writes=[psn])
                S.op(ACT, lambda e, ps=ps, j=j: e.activation(out=Q[:, j, :], in_=ps[:], func=AF.Identity, scale=1.0 / 16.0), reads=[psn], writes=["HID%d" % j])
            self.tick()
        def scores(h):
            eb = h % 2
            for mc in range(2):
                ps, psn = self.bank()
                for dd in range(2):
                    dc = 2 * h + dd
                    S.op(PE, lambda e, ps=ps, dc=dc, mc=mc, dd=dd: e.matmul(ps[:], lhsT=self.KT[:, l, dc, mc * 128:(mc + 1) * 128], rhs=Q[:, dc, :], start=(dd == 0), stop=(dd == 1)),
                         reads=["KT%d" % l, "HID%d" % dc], writes=[psn])
                S.op(ACT, lambda e, ps=ps, mc=mc, eb=eb: e.activation(out=self.E[:, eb, mc, :], in_=ps[:], func=AF.Exp), reads=[psn], writes=["E%d_%d" % (eb, mc)])

        def finish(h):
            eb = h % 2
            sums, sumn = self.stbank()
            for mc in range(2):
                S.op(PE, lambda e, mc=mc, eb=eb, sums=sums: e.matmul(sums[:], lhsT=self.ONESB[:], rhs=self.E[:, eb, mc, :], start=(mc == 0), stop=(mc == 1)),
                     reads=["E%d_%d" % (eb, mc), "ONESB"], writes=[sumn])
            R, rn = self.rs()
            S.op(ACT, lambda e, R=R, sums=sums: e.activation(out=R[:], in_=sums[:], func=AF.Ln), reads=[sumn], writes=[rn])
            S.op(ACT, lambda e, R=R: e.activation(out=R[:], in_=R[:], func=AF.Exp, scale=-1.0), reads=[rn], writes=[rn])
            for dd in range(2):
                dc = 2 * h + dd
                ps, psn = self.bank()
                for mc in range(2):
                    S.op(PE, lambda e, ps=ps, mc=mc, dc=dc, eb=eb: e.matmul(ps[:], lhsT=self.V[:, l, mc, dc * 128:(dc + 1) * 128], rhs=self.E[:, eb, mc, :], start=(mc == 0), stop=(mc == 1)),
                         reads=["V%d" % l, "E%d_%d" % (eb, mc)], writes=[psn])
                S.op(DVE, lambda e, ps=ps, dc=dc, R=R: e.tensor_tensor(out=Q[:, 8 + dc, :], in0=ps[:], in1=R[:], op=ALU.mult), reads=[psn, rn], writes=["HID%d" % (8 + dc)])

        scores(0)
        for h in range(4):
            if h + 1 < 4:
                scores(h + 1)
            finish(h)
            self.tick()
        self.proj_out(s, "wo", l, 4, [(Q[:, 8 + k, :], "HID%d" % (8 + k)) for k in range(8)], "xpost", l)

    def ln_bcast(self, src_chunks, nch):
        S = self.S
        s1, s1n = self.stbank()
        s2, s2n = self.stbank()
        n = float(nch * 128)
        for c, (ap, rn_) in enumerate(src_chunks):
            S.op(PE, lambda e, c=c, ap=ap: e.matmul(s1[:], lhsT=self.ONESF[:], rhs=ap, start=(c == 0), stop=(c == nch - 1)), reads=[rn_, "ONESF"], writes=[s1n])
        for c, (ap, rn_) in enumerate(src_chunks):
            sg, sgn = self.sg()
            S.op(ACT, lambda e, ap=ap, sg=sg: e.activation(out=sg[:], in_=ap, func=AF.Square), reads=[rn_], writes=[sgn])
            S.op(PE, lambda e, c=c, sg=sg: e.matmul(s2[:], lhsT=self.ONESF[:], rhs=sg[:], start=(c == 0), stop=(c == nch - 1)), reads=[sgn, "ONESF"], writes=[s2n])
        MEAN, mn = self.rs()
        VAR, vn = self.rs()
        S.op(DVE, lambda e: e.tensor_scalar(out=MEAN[:], in0=s1[:], scalar1=1.0 / n, scalar2=None, op0=ALU.mult), reads=[s1n], writes=[mn])
        S.op(DVE, lambda e: e.tensor_tensor(out=VAR[:], in0=MEAN[:], in1=MEAN[:], op=ALU.mult), reads=[mn], writes=[vn])
        S.op(DVE, lambda e: e.scalar_tensor_tensor(out=VAR[:], in0=s2[:], scalar=1.0 / n, in1=VAR[:], op0=ALU.mult, op1=ALU.subtract), reads=[s2n, vn], writes=[vn])
        S.op(ACT, lambda e: e.activation(out=VAR[:], in_=VAR[:], func=AF.Ln, bias=EPS, scale=1.0), reads=[vn], writes=[vn])
        S.op(ACT, lambda e: e.activation(out=VAR[:], in_=VAR[:], func=AF.Exp, scale=-0.5), reads=[vn], writes=[vn])
        S.op(DVE, lambda e: e.tensor_tensor(out=MEAN[:], in0=MEAN[:], in1=VAR[:], op=ALU.mult), reads=[mn, vn], writes=[mn])
        return VAR, vn, MEAN, mn

    def mixer_even(self, s, l, t):
        S = self.S
        d = self.d
        ev = l // 2
        Z = self.SCR
        ZB = self.ZB
        CA = self.F
        Y = self.HID
        for c in range(4):
            S.op(POOL, lambda e, c=c: e.tensor_copy(out=ZB[:, c, 0:30], in_=self.CARZ[:, ev, c, :]), reads=["CARZ%d" % ev], writes=["ZB%d" % c])
        for c in range(4):
            sl, sn = self.WS.next(S, self, "evin", (ev, c), 2048)
            g, gn = self.bank()
            v, vn = self.bank()
            for k in range(8):
                S.op(PE, lambda e, g=g, sl=sl, k=k: e.matmul(g[:], lhsT=self.WB[:, sl, k * 128:(k + 1) * 128], rhs=self.XN[s][:, k, :], start=(k == 0), stop=(k == 7)),
                     reads=[sn, "XN%d_%d" % (s, k)], writes=[gn])
            for k in range(8):
                S.op(PE, lambda e, v=v, sl=sl, k=k: e.matmul(v[:], lhsT=self.WB[:, sl, 1024 + k * 128:1024 + (k + 1) * 128], rhs=self.XN[s][:, k, :], start=(k == 0), stop=(k == 7)),
                     reads=[sn, "XN%d_%d" % (s, k)], writes=[vn])
            sg, sgn = self.sg()
            S.op(ACT, lambda e, g=g, sg=sg: e.activation(out=sg[:], in_=g[:], func=AF.Tanh, scale=0.5), reads=[gn], writes=[sgn])
            S.op(DVE, lambda e, v=v, sg=sg, c=c: e.scalar_tensor_tensor(out=ZB[:, c, 30:30 + TT], in0=sg[:], scalar=1.0, in1=v[:], op0=ALU.add, op1=ALU.mult),
                 reads=[sgn, vn], writes=["ZB%d" % c])
            self.tick()
        self.flush_post()
        for uu in range(4):
            sl, sn = self.WS.next(S, self, "evin", (ev, 4 + uu), 2048)
            for jj in range(2):
                cc = (uu % 2) * 2 + jj
                ps, psn = self.bank()
                for k in range(8):
                    off = (jj * 8 + k) * 128
                    S.op(PE, lambda e, ps=ps, sl=sl, off=off, k=k: e.matmul(ps[:], lhsT=self.WB[:, sl, off:off + 128], rhs=self.XN[s][:, k, :], start=(k == 0), stop=(k == 7)),
                         reads=[sn, "XN%d_%d" % (s, k)], writes=[psn])
                if uu < 2:
                    S.op(ACT, lambda e, ps=ps, cc=cc: e.activation(out=CA[:, 4 + cc, :], in_=ps[:], func=AF.Gelu), reads=[psn], writes=["F%d" % (4 + cc)])
                else:
                    S.op(ACT, lambda e, ps=ps, cc=cc: e.activation(out=Z[:, 4 + cc, 0:TT], in_=ps[:], func=AF.Gelu), reads=[psn], writes=["SCR%d" % (4 + cc)])
            self.tick()
        RSTD, rn, MR, mrn = self.ln_bcast([(Z[:, 4 + c, 0:TT], "SCR%d" % (4 + c)) for c in range(4)], 4)
        for c in range(4):
            S.op(DVE, lambda e, c=c: e.tensor_tensor(out=Z[:, 4 + c, 0:TT], in0=Z[:, 4 + c, 0:TT], in1=RSTD[:], op=ALU.mult), reads=["SCR%d" % (4 + c), rn], writes=["SCR%d" % (4 + c)])
            S.op(DVE, lambda e, c=c: e.tensor_tensor(out=Z[:, 4 + c, 0:TT], in0=Z[:, 4 + c, 0:TT], in1=MR[:], op=ALU.subtract), reads=["SCR%d" % (4 + c), mrn], writes=["SCR%d" % (4 + c)])
            S.op(DVE, lambda e, c=c: e.tensor_scalar(out=Y[:, 8 + c, :], in0=Z[:, 4 + c, 0:TT], scalar1=self.vcol("slg", ev, c), scalar2=self.vcol("slb", ev, c), op0=ALU.mult, op1=ALU.add),
                 reads=["SCR%d" % (4 + c), "VEC"], writes=["HID%d" % (8 + c)])
        for c in range(4):
            ps, psn = self.bank()
            for hf in range(2):
                sl, sn = self.WS.next(S, self, "evcv", (ev, 2 * c + hf), 2048)
                for jl in range(16 if hf == 0 else 15):
                    j = hf * 16 + jl
                    S.op(PE, lambda e, ps=ps, sl=sl, jl=jl, j=j, c=c: e.matmul(ps[:], lhsT=self.WB[:, sl, jl * 128:(jl + 1) * 128], rhs=ZB[:, c, j:j + TT], start=(j == 0), stop=(j == CW - 1)),
                         reads=[sn, "ZB%d" % c], writes=[psn])
            S.op(ACT, lambda e, ps=ps, c=c: e.activation(out=CA[:, c, :], in_=ps[:], func=AF.Identity, scale=0.5, bias=self.vcol("cb", ev, c)), reads=[psn, "VEC"], writes=["F%d" % c])
            self.tick()
        S.op(POOL, lambda e: e.tensor_copy(out=self.CARZ[:, ev, :, :], in_=ZB[:, 0:4, TT:TT + 30]), reads=["ZB%d" % c for c in range(4)], writes=["CARZ%d" % ev])
        for tc in range(4):
            for h in range(4):
                S.op(PE, lambda e, tc=tc, h=h: e.transpose(self.TPB[:, h * 128:(h + 1) * 128], Y[:, 8 + h, tc * 128:(tc + 1) * 128], self.IDB[:]),
                     reads=["HID%d" % (8 + h), "IDB"], writes=["TPB"])
            S.op(ACT, lambda e, tc=tc: e.activation(out=Y[:, 12 + tc, :], in_=self.TPB[:, 0:512], func=AF.Identity), reads=["TPB"], writes=["HID%d" % (12 + tc)])
        for h in range(4):
            ps, psn = self.bank()
            for tc in range(4):
                S.op(PE, lambda e, ps=ps, tc=tc, h=h: e.matmul(ps[:, tc * 128:(tc + 1) * 128], lhsT=Y[:, 12 + tc, h * 128:(h + 1) * 128], rhs=self.WST[:, ev, h * 128:(h + 1) * 128], start=True, stop=True),
                     reads=["HID%d" % (12 + tc), "WST"], writes=[psn])
            sg, sgn = self.sg()
            for tc in range(4):
                S.op(DVE, lambda e, ps=ps, sg=sg, tc=tc, h=h: e.tensor_tensor(out=sg[:, tc * 128:(tc + 1) * 128], in0=ps[:, tc * 128:(tc + 1) * 128], in1=self.BSB[:, ev, h * 128:(h + 1) * 128], op=ALU.add),
                     reads=[psn, "BSB"], writes=[sgn])
            S.op(DVE, lambda e, sg=sg, h=h: e.tensor_tensor(out=Y[:, 4 + h, :], in0=sg[:], in1=CA[:, 4 + h, :], op=ALU.mult), reads=[sgn, "F%d" % (4 + h)], writes=["HID%d" % (4 + h)])
            self.tick()
        RSTD2, rn2, MR2, mrn2 = self.ln_bcast([(CA[:, c, :], "F%d" % c) for c in range(4)], 4)
        for c in range(4):
            S.op(DVE, lambda e, c=c: e.tensor_tensor(out=CA[:, c, :], in0=CA[:, c, :], in1=RSTD2[:], op=ALU.mult), reads=["F%d" % c, rn2], writes=["F%d" % c])
            S.op(DVE, lambda e, c=c: e.tensor_tensor(out=CA[:, c, :], in0=CA[:, c, :], in1=MR2[:], op=ALU.subtract), reads=["F%d" % c, mrn2], writes=["F%d" % c])
            S.op(ACT, lambda e, c=c: e.activation(out=Y[:, c, :], in_=CA[:, c, :], func=AF.Silu, scale=self.vcol("clg", ev, c), bias=self.vcol("clb", ev, c)),
                 reads=["F%d" % c, "VEC"], writes=["HID%d" % c])
        self.proj_out(s, "evout", ev, 4, [(Y[:, k, :], "HID%d" % k) for k in range(8)], "mpost", l)

    def mixer_odd(self, s, l, t):
        S = self.S
        d = self.d
        od = l // 2
        PP = self.SCR
        Dm = self.HID
        for c in range(8):
            S.op(POOL, lambda e, c=c: e.tensor_copy(out=PP[:, c, 0:16], in_=self.CARP[:, od, c, :]), reads=["CARP%d" % od], writes=["SCR%d" % c])
        def cascade(c):
            g = c // 2
            w = 2 << g
            eng = DVE if c % 2 == 0 else POOL
            ta, tb = (0, 1) if c % 2 == 0 else (2, 3)
            tan, tbn = ["F%d" % (2 * ta), "F%d" % (2 * ta + 1)], ["F%d" % (2 * tb), "F%d" % (2 * tb + 1)]
            TA = self.F[:, 2 * ta:2 * ta + 2, :].rearrange("p a b -> p (a b)")
            TB = self.F[:, 2 * tb:2 * tb + 2, :].rearrange("p a b -> p (a b)")
            src, srcn = PP[:, c, :], "SCR%d" % c
            sh = 1
            cur, curn = None, None
            for step in range(g + 1):
                dst, dstn = (TA, tan) if step % 2 == 0 else (TB, tbn)
                lo = 2 * sh - 1
                inp, inpn = (src, [srcn]) if step == 0 else (cur, curn)
                S.op(eng, lambda e, dst=dst, inp=inp, lo=lo, sh=sh: e.tensor_tensor(out=dst[:, lo:528], in0=inp[:, lo:528], in1=inp[:, lo - sh:528 - sh], op=ALU.add),
                     reads=inpn, writes=dstn)
                cur, curn = dst, dstn
                sh *= 2
            S.op(DVE, lambda e, cur=cur, c=c, w=w: e.scalar_tensor_tensor(out=Dm[:, c, :], in0=cur[:, 16:528], scalar=1.0 / w, in1=PP[:, c, 16:528], op0=ALU.mult, op1=ALU.subtract),
                 reads=curn + ["SCR%d" % c], writes=["HID%d" % c])
            if t == 0:
                sg, sgn = self.sg()
                S.op(eng, lambda e, cur=cur, sg=sg, g=g: e.tensor_tensor(out=sg[:, 0:16], in0=cur[:, 16:32], in1=self.CST[:, C_INV + g * 16:C_INV + (g + 1) * 16], op=ALU.mult),
                     reads=curn + ["CST"], writes=[sgn])
                S.op(eng, lambda e, sg=sg, c=c: e.tensor_tensor(out=Dm[:, c, 0:16], in0=sg[:, 0:16], in1=PP[:, c, 16:32], op=ALU.subtract),
                     reads=[sgn, "SCR%d" % c, "HID%d" % c], writes=["HID%d" % c])
        self.flush_post()
        for u in (3, 2, 1, 0):
            sl, sn = self.WS.next(S, self, "odin", (od, u), 2048)
            for jj in range(2):
                j = 2 * u + jj
                ps, psn = self.bank()
                for k in range(8):
                    off = (jj * 8 + k) * 128
                    S.op(PE, lambda e, ps=ps, sl=sl, off=off, k=k: e.matmul(ps[:], lhsT=self.WB[:, sl, off:off + 128], rhs=self.XN[s][:, k, :], start=(k == 0), stop=(k == 7)),
                         reads=[sn, "XN%d_%d" % (s, k)], writes=[psn])
                S.op(ACT, lambda e, ps=ps, j=j: e.activation(out=PP[:, j, 16:16 + TT], in_=ps[:], func=AF.Identity), reads=[psn], writes=["SCR%d" % j])
            cascade(2 * u)
            cascade(2 * u + 1)
            self.tick()
        S.op(POOL, lambda e: e.tensor_copy(out=self.CARP[:, od, :, :], in_=PP[:, :, TT:TT + 16]), reads=["SCR%d" % c for c in range(8)], writes=["CARP%d" % od])
        sl, sn = self.WS.next(S, self, "odg", (od,), 2048)
        for g in (3, 2, 1, 0):
            for jn in range(2):
                ps, psn = self.bank()
                for kc in range(2):
                    off = g * 512 + kc * 256 + jn * 128
                    S.op(PE, lambda e, ps=ps, sl=sl, off=off, kc=kc, g=g: e.matmul(ps[:], lhsT=self.WB[:, sl, off:off + 128], rhs=Dm[:, 2 * g + kc, :], start=(kc == 0), stop=(kc == 1)),
                         reads=[sn, "HID%d" % (2 * g + kc)], writes=[psn])
                cc = 2 * g + jn
                S.op(ACT, lambda e, ps=ps, cc=cc: e.activation(out=Dm[:, 8 + cc, :], in_=ps[:], func=AF.Identity, scale=self.vcol("osc", od, cc)), reads=[psn, "VEC"], writes=["HID%d" % (8 + cc)])
            self.tick()
        self.proj_out(s, "odout", od, 4, [(Dm[:, 8 + k, :], "HID%d" % (8 + k)) for k in range(8)], "mpost", l)


def build_program(ntiles=8, layers=(0, 1, 2, 3), debug=False):
    nc = bass.Bass("TRN2", target_bir_lowering=False)
    B = Builder(nc, ntiles, list(layers), debug)
    st = contextlib.ExitStack()
    with st:
        B.alloc(st)
        ws0 = WStream(None)
        B.emit_all(Sched(), ws0)
        B.out_toks = []
        S = Sched()
        B.emit_all(S, WStream(ws0.rec))
        S.emit(nc, st)
    return nc


def blk2(W, K):
    N = W.shape[1]
    a = W.reshape(K, 128, N // 256, 2, 128)
    return np.ascontiguousarray(a.transpose(2, 1, 3, 0, 4)).reshape(N // 256, 128, 2 * K * 128)


def colvec(v):
    n = v.shape[0] // 128
    return np.ascontiguousarray(v.reshape(n, 128).T)


def host_weights(inp):
    f = lambda a: np.asarray(a, dtype=np.float32)
    w = {}
    for which in (1, 2):
        gu = f({1: inp["ffn1_w_gu"], 2: inp["ffn2_w_gu"]}[which])
        g = gu[:, :, :DFF].reshape(4, 8, 128, 22, 128)
        u = gu[:, :, DFF:].reshape(4, 8, 128, 22, 128)
        a = np.stack([g, u], axis=0)
        w["wgu%d" % which] = np.ascontiguousarray(a.transpose(1, 4, 3, 0, 2, 5)).reshape(4, 22, 128, 2048)
        dn = f({1: inp["ffn1_w_down"], 2: inp["ffn2_w_down"]}[which]).reshape(4, 22, 128, 8, 128)
        w["wd%d" % which] = np.ascontiguousarray(dn.transpose(0, 3, 2, 1, 4)).reshape(4, 8, 128, 2816)
    evin = f(inp["ev_w_in"])
    units = []
    for e in range(2):
        W = evin[e].reshape(8, 128, 16, 128)
        us = []
        for c in range(4):
            us.append(np.stack([W[:, :, 4 + c, :], W[:, :, c, :]], axis=0))
        for b0 in (8, 10, 12, 14):
            us.append(np.stack([W[:, :, b0, :], W[:, :, b0 + 1, :]], axis=0))
        units.append(np.stack([np.ascontiguousarray(x.transpose(2, 0, 1, 3)).reshape(128, 2048) for x in us], axis=0))
    w["evin"] = np.stack(units, axis=0)
    w["evout"] = np.stack([blk2(f(inp["ev_w_out"])[e], 8) for e in range(2)], axis=0)
    cwf = f(inp["ev_conv_w"])
    evcv = np.zeros((2, 8, 128, 16, 128), np.float32)
    pidx = np.arange(128)
    for e in range(2):
        for c in range(4):
            for j in range(CW):
                evcv[e, 2 * c + j // 16, pidx, j % 16, pidx] = cwf[e, j, c * 128:(c + 1) * 128]
    w["evcv"] = evcv.reshape(2, 8, 128, 2048)
    w["odin"] = np.stack([blk2(f(inp["od_w_in"])[o], 8) for o in range(2)], axis=0)
    w["odout"] = np.stack([blk2(f(inp["od_w_out"])[o], 8) for o in range(2)], axis=0)
    og = f(inp["od_w_group"]).reshape(2, 4, 2, 128, 256)
    w["odg"] = np.ascontiguousarray(og.transpose(0, 3, 1, 2, 4)).reshape(2, 128, 2048)
    w["wq"] = np.stack([blk2(f(inp["xa_w_q"])[l], 8) for l in range(4)], axis=0)
    w["wo"] = np.stack([blk2(f(inp["xa_w_o"])[l], 8) for l in range(4)], axis=0)
    kv = f(inp["xa_w_kv"])
    w["wk"] = np.stack([blk2(kv[l][:, :D], 8) for l in range(4)], axis=0)
    wv = kv[:, :, D:].reshape(4, 8, 128, 4, 256)
    w["wv"] = np.ascontiguousarray(wv.transpose(0, 3, 2, 1, 4)).reshape(4, 4, 128, 2048)
    vec = np.zeros((128, NVEC), np.float32)
    names = {"f1pre": "ffn1_pre_g", "f1post": "ffn1_post_g", "mpre": "mix_pre_g", "mpost": "mix_post_g", "xpre": "xa_pre_g",
             "xmem": "xa_mem_g", "xpost": "xa_post_g", "f2pre": "ffn2_pre_g", "f2post": "ffn2_post_g"}
    for l in range(4):
        for k, nm in names.items():
            o = VCOLS[(k, l)]
            vec[:, o:o + 8] = colvec(f(inp[nm])[l])
    en = {"cb": "ev_conv_b", "clg": "ev_conv_ln_g", "clb": "ev_conv_ln_b", "slg": "ev_sgu_ln_g", "slb": "ev_sgu_ln_b"}
    for e in range(2):
        for k, nm in en.items():
            o = VCOLS[(k, e)]
            vec[:, o:o + 4] = colvec(f(inp[nm])[e])
        cw = f(inp["ev_conv_w"])[e]
        o = VCOLS[("cw", e)]
        vec[:, o:o + 4 * CW] = np.ascontiguousarray(cw.T.reshape(4, 128, CW).transpose(1, 0, 2)).reshape(128, 4 * CW)
    for o_ in range(2):
        o = VCOLS[("osc", o_)]
        vec[:, o:o + 8] = colvec(f(inp["od_scale"])[o_])
    w["vec"] = vec
    cst = np.zeros((128, NCST), np.float32)
    s = np.arange(128)
    cst[:, C_MASK:C_MASK + 128] = (s[:, None] <= s[None, :]).astype(np.float32)
    cst[:, C_ID:C_ID + 128] = np.eye(128, dtype=np.float32)
    for g, wd in enumerate((2, 4, 8, 16)):
        cst[:, C_INV + g * 16:C_INV + (g + 1) * 16] = 1.0 / np.minimum(np.arange(1, 17), wd)[None, :]
    w["cst"] = cst
    sw = f(inp["ev_sgu_w"])
    w["wst"] = np.ascontiguousarray(sw.transpose(0, 3, 1, 2)).reshape(2, 128, 512)
    sb_ = f(inp["ev_sgu_b"]).reshape(2, 1, 512)
    w["bsb"] = np.ascontiguousarray(np.broadcast_to(sb_, (2, 128, 512)))
    return w


_PROG = {}


def run(inputs, ntiles=8, layers=(0, 1, 2, 3), batches=None, x_override=None, debug=False):
    key = (ntiles, tuple(layers), debug)
    if key not in _PROG:
        _PROG[key] = build_program(ntiles, layers, debug)
    nc = _PROG[key]
    w = host_weights(inputs)
    x = np.asarray(inputs["x"] if x_override is None else x_override, dtype=np.float32)
    mem = np.asarray(inputs["mem"], dtype=np.float32)
    nb = x.shape[0] if batches is None else batches
    in_maps = []
    for b in range(nb):
        m = dict(w)
        m["xT"] = np.ascontiguousarray(x[b, :ntiles * TT, :].T)
        m["memT"] = np.ascontiguousarray(mem[b].T)
        in_maps.append(m)
    res = run_bass_kernel_spmd(nc, in_maps, core_ids=list(range(nb)))
    out = np.stack([np.ascontiguousarray(r["yT"].T) for r in res.results], axis=0)
    return out


def kernel(**inputs):
    return run(inputs).astype(np.float32)
```

```python
import contextlib
import numpy as np
import concourse.bass as bass
import concourse.mybir as mybir
from concourse.bass_utils import run_bass_kernel_spmd

F32 = mybir.dt.float32
BF16 = mybir.dt.bfloat16
AF = mybir.ActivationFunctionType
ALU = mybir.AluOpType

PE, ACT, DVE, POOL, SP = "pe", "act", "dve", "pool", "sp"
ENGS = (PE, ACT, DVE, POOL, SP)

D = 1024
DFF = 2816
SEQ = 4096
TT = 512
NMEM = 256
NSLOT = 5
SLOT = 2816
EPS = 1e-6
CW = 31
NRS = 2


class Sched:
    def __init__(self, strict_same=(ACT, DVE, POOL)):
        self.q = {e: [] for e in ENGS}
        self.last_w = {}
        self.readers = {}
        self.waited = {e: {} for e in ENGS}
        self.dma_cnt = {}
        self.strict_same = set(strict_same)

    def _need(self, eng, tok, waits):
        if tok is None:
            return
        if tok[0] == "e":
            _, peng, idx = tok
            if peng == eng and eng not in self.strict_same:
                return
            key = ("e", peng)
            if self.waited[eng].get(key, -1) >= idx:
                return
            self.waited[eng][key] = idx
            self.q[peng][idx]["inc"] = True
            waits.append(tok)
        else:
            _, sk, val = tok
            key = ("d", sk)
            if self.waited[eng].get(key, -1) >= val:
                return
            self.waited[eng][key] = val
            waits.append(tok)

    def _deps(self, eng, reads, writes):
        waits = []
        for r in reads:
            self._need(eng, self.last_w.get(r), waits)
        for w in writes:
            self._need(eng, self.last_w.get(w), waits)
            for t in self.readers.get(w, ()):
                self._need(eng, t, waits)
        return waits

    def _post(self, tok, reads, writes):
        for r in reads:
            self.readers.setdefault(r, []).append(tok)
        for w in writes:
            self.last_w[w] = tok
            self.readers[w] = []

    def op(self, eng, fn, reads=(), writes=()):
        waits = self._deps(eng, reads, writes)
        idx = len(self.q[eng])
        self.q[eng].append({"fn": fn, "waits": waits, "inc": False, "dma": None})
        tok = ("e", eng, idx)
        self._post(tok, reads, writes)
        return tok

    def dma(self, eng, fn, semkey, reads=(), writes=()):
        waits = self._deps(eng, reads, writes)
        n = self.dma_cnt.get(semkey, 0) + 1
        self.dma_cnt[semkey] = n
        self.q[eng].append({"fn": fn, "waits": waits, "inc": False, "dma": semkey})
        tok = ("d", semkey, 16 * n)
        self._post(tok, reads, writes)
        return tok

    def final_wait(self, eng, toks):
        waits = []
        for t in toks:
            self._need(eng, t, waits)
        self.q[eng].append({"fn": None, "waits": waits, "inc": False, "dma": None})

    def emit(self, nc, stack):
        esem = {e: stack.enter_context(nc.semaphore("s_" + e)) for e in ENGS}
        dsem = {k: stack.enter_context(nc.semaphore("d_%s" % str(k))) for k in self.dma_cnt}
        cum = {}
        for e in ENGS:
            c = 0
            arr = []
            for o in self.q[e]:
                if o["inc"]:
                    c += 1
                arr.append(c)
            cum[e] = arr
        block = stack.enter_context(nc.Block())

        def run(e, engobj):
            for o in self.q[e]:
                for t in o["waits"]:
                    if t[0] == "e":
                        engobj.wait_ge(esem[t[1]], cum[t[1]][t[2]])
                    else:
                        engobj.wait_ge(dsem[t[1]], t[2])
                if o["fn"] is None:
                    continue
                ins = o["fn"](engobj)
                if o["dma"] is not None:
                    ins.then_inc(dsem[o["dma"]], 16)
                elif o["inc"]:
                    ins.then_inc(esem[e], 1)

        @block.tensor
        def _(eng):
            run(PE, eng)

        @block.scalar
        def _(eng):
            run(ACT, eng)

        @block.vector
        def _(eng):
            run(DVE, eng)

        @block.gpsimd
        def _(eng):
            run(POOL, eng)

        @block.sync
        def _(eng):
            run(SP, eng)


def vec_layout():
    cols = {}
    n = 0
    for l in range(4):
        for nm in ("f1pre", "f1post", "mpre", "mpost", "xpre", "xmem", "xpost", "f2pre", "f2post"):
            cols[(nm, l)] = n
            n += 8
    for e in range(2):
        for nm in ("cb", "clg", "clb", "slg", "slb"):
            cols[(nm, e)] = n
            n += 4
        cols[("cw", e)] = n
        n += 4 * CW
    for o in range(2):
        cols[("osc", o)] = n
        n += 8
    return cols, n


VCOLS, NVEC = vec_layout()
C_MASK, C_ID, C_INV, NCST = 0, 128, 256, 320


class WStream:
    def __init__(self, plan=None):
        self.plan = plan
        self.rec = []
        self.i = 0
        self.issued = 0
        self.seen = set()

    def next(self, S, B, name, idx, E):
        i = self.i
        self.i += 1
        if self.plan is None:
            self.rec.append((name, idx, E))
            return i % NSLOT, "ws%d" % (i % NSLOT)
        WB = B.WB
        upto = min(i + NSLOT - 1, len(self.plan) - 1)
        while self.issued <= upto:
            u = self.issued
            nm, ix, uE = self.plan[u]
            sl = u % NSLOT
            key = (nm, ix)
            if key not in self.seen:
                self.seen.add(key)
                src = B.d[nm][ix]
                S.dma(POOL, lambda e, sl=sl, src=src, uE=uE: e.dma_start(out=WB[:, sl, 0:uE], in_=src[:, 0:uE]),
                      "w%d" % sl, writes=["ws%d" % sl])
                if nm in B.scr:
                    dst = B.scr[nm][ix]
                    S.dma(SP, lambda e, sl=sl, dst=dst, uE=uE: e.dma_start(out=dst[:, 0:uE], in_=WB[:, sl, 0:uE]),
                          "ss%d" % sl, reads=["ws%d" % sl], writes=[("scr", nm, ix)])
            else:
                src = B.scr[nm][ix]
                S.dma(POOL, lambda e, sl=sl, src=src, uE=uE: e.dma_start(out=WB[:, sl, 0:uE], in_=src[:, 0:uE]),
                      "w%d" % sl, reads=[("scr", nm, ix)], writes=["ws%d" % sl])
            self.issued += 1
        return i % NSLOT, "ws%d" % (i % NSLOT)


class Builder:
    def __init__(self, nc, ntiles, layers, debug=False):
        self.nc = nc
        self.ntiles = ntiles
        self.layers = layers
        dt = nc.dram_tensor
        self.d = {}
        for name, shape in self.dram_specs(ntiles).items():
            self.d[name] = dt(name, list(shape), F32, kind="ExternalInput").ap()
        self.yT = dt("yT", [D, ntiles * TT], F32, kind="ExternalOutput").ap()
        self.scr = {}
        for name, shape in self.dram_specs(ntiles).items():
            if name in ("wgu1", "wgu2", "wd1", "wd2", "evin", "evcv", "evout", "odin", "odg", "odout", "wq", "wo"):
                self.scr[name] = dt(name + "_b", list(shape), BF16).ap()

    @staticmethod
    def dram_specs(ntiles):
        return {
            "xT": (D, ntiles * TT), "memT": (D, NMEM),
            "wgu1": (4, 22, 128, 2048), "wgu2": (4, 22, 128, 2048),
            "wd1": (4, 8, 128, 2816), "wd2": (4, 8, 128, 2816),
            "evin": (2, 8, 128, 2048), "evcv": (2, 8, 128, 2048), "evout": (2, 4, 128, 2048),
            "odin": (2, 4, 128, 2048), "odg": (2, 128, 2048), "odout": (2, 4, 128, 2048),
            "wq": (4, 4, 128, 2048), "wo": (4, 4, 128, 2048), "wk": (4, 4, 128, 2048), "wv": (4, 4, 128, 2048),
            "vec": (128, NVEC), "cst": (128, NCST), "wst": (2, 128, 512), "bsb": (2, 128, 512),
        }

    def alloc(self, st):
        nc = self.nc
        sb = lambda n, s, t: st.enter_context(nc.sbuf_tensor(n, s, t))
        self.H = [sb("H0", [128, 8, TT], F32), sb("H1", [128, 8, TT], F32)]
        self.XN = [sb("XN0", [128, 8, TT], BF16), sb("XN1", [128, 8, TT], BF16)]
        self.HID = sb("HID", [128, 22, TT], BF16)
        self.F = sb("F", [128, 8, TT], F32)
        self.SQ = sb("SQ", [128, 8, TT], BF16)
        self.RS = [sb("RS%d" % i, [128, TT], F32) for i in range(NRS)]
        self.RD = [sb("RD%d" % i, [128, TT], F32) for i in range(4)]
        self.SG = [sb("SG%d" % i, [128, TT], F32) for i in range(2)]
        self.WB = sb("WB", [128, NSLOT, SLOT], BF16)
        self.KT = sb("KT", [128, 4, 8, NMEM], BF16)
        self.V = sb("V", [128, 4, 2, D], BF16)
        self.SCR = sb("SCR", [128, 8, 544], F32)
        self.E = sb("E", [128, 2, 2, TT], BF16)
        self.CARZ = sb("CARZ", [128, 2, 4, 30], BF16)
        self.ZB = sb("ZB", [128, 4, 544], BF16)
        self.CARP = sb("CARP", [128, 2, 8, 16], F32)
        self.VEC = sb("VEC", [128, NVEC], F32)
        self.CST = sb("CST", [128, NCST], F32)
        self.ONESB = sb("ONESB", [128, 128], BF16)
        self.ONESF = sb("ONESF", [128, 128], F32)
        self.IDB = sb("IDB", [128, 128], BF16)
        self.WST = sb("WST", [128, 2, 512], BF16)
        self.BSB = sb("BSB", [128, 2, 512], F32)
        self.PS = [st.enter_context(nc.psum_tensor("PS%d" % i, [128, TT], F32)) for i in range(7)]
        self.TPB = st.enter_context(nc.psum_tensor("TPB", [128, 1024], BF16))
        self.pi = 0
        self.sti = 0
        self.rsi = 0
        self.sgi = 0

    def bank(self):
        i = self.pi % 6
        self.pi += 1
        return self.PS[i], "PS%d" % i

    def stbank(self):
        return self.bank()

    def chainbank(self, s):
        return self.PS[6], "PS6"

    def rs(self):
        i = self.rsi % NRS
        self.rsi += 1
        return self.RS[i], "RS%d" % i

    def sg(self):
        i = self.sgi % 2
        self.sgi += 1
        return self.SG[i], "SG%d" % i

    def vcol(self, nm, idx, c):
        o = VCOLS[(nm, idx)] + c
        return self.VEC[:, o:o + 1]

    def emit_all(self, S, WS):
        self.S, self.WS = S, WS
        self.pi = self.sti = self.rsi = self.sgi = 0
        self.pending = []
        self.postR = {}
        self.preR = {}
        d = self.d
        S.dma(SP, lambda e: e.dma_start(out=self.VEC[:], in_=d["vec"]), "par", writes=["VEC"])
        S.dma(SP, lambda e: e.dma_start(out=self.CST[:], in_=d["cst"]), "par2", writes=["CST"])
        S.dma(SP, lambda e: e.dma_start(out=self.BSB[:], in_=d["bsb"].rearrange("e p f -> p e f")), "par3", writes=["BSB"])
        S.op(DVE, lambda e: e.memset(self.ONESB[:], 1.0), writes=["ONESB"])
        S.op(DVE, lambda e: e.memset(self.ONESF[:], 1.0), writes=["ONESF"])
        S.op(DVE, lambda e: e.memset(self.CARZ[:], 0.0), writes=["CARZ0", "CARZ1"])
        S.op(DVE, lambda e: e.memset(self.CARP[:], 0.0), writes=["CARP0", "CARP1"])
        S.op(DVE, lambda e: e.tensor_copy(out=self.IDB[:], in_=self.CST[:, C_ID:C_ID + 128]), reads=["CST"], writes=["IDB"])
        for l in range(4):
            for nm in ("f1post", "f2post"):
                o = VCOLS[(nm, l)]
                S.op(DVE, lambda e, o=o: e.tensor_scalar(out=self.VEC[:, o:o + 8], in0=self.VEC[:, o:o + 8], scalar1=0.5, scalar2=None, op0=ALU.mult),
                     reads=["VEC"], writes=["VEC"])
        for ev in range(2):
            S.dma(SP, lambda e, ev=ev: e.dma_start(out=self.SG[0][:], in_=d["wst"][ev]), "par4", writes=["SG0"])
            for h in range(4):
                S.op(DVE, lambda e, ev=ev, h=h: e.tensor_tensor(out=self.WST[:, ev, h * 128:(h + 1) * 128], in0=self.SG[0][:, h * 128:(h + 1) * 128],
                                                                in1=self.CST[:, C_MASK:C_MASK + 128], op=ALU.mult),
                     reads=["SG0", "CST"], writes=["WST"])
        for l in self.layers:
            self.kv_prologue(l)
        subl = [(kind, l) for l in self.layers for kind in ("f1", "mix", "xa", "f2")]
        prename = {"f1": "f1pre", "mix": "mpre", "xa": "xpre", "f2": "f2pre"}
        self.pending = []
        self.out_toks = []
        nstream = 2 if self.ntiles >= 2 else 1
        for tp in range(self.ntiles // nstream):
            for s in range(nstream):
                t = tp * nstream + s
                S.dma(SP, lambda e, t=t, s=s: e.dma_start(out=self.H[s][:], in_=d["xT"][:, t * TT:(t + 1) * TT].rearrange("(c p) t -> p c t", p=128)),
                      "hin%d" % s, writes=["H%d_%d" % (s, c) for c in range(8)])
            for s in range(nstream):
                self.pre_norm(s, prename[subl[0][0]], subl[0][1])
            for k, (kind, l) in enumerate(subl):
                for s in range(nstream):
                    t = tp * nstream + s
                    self.flush(s)
                    if kind == "f1":
                        self.ffn(s, l, 1)
                    elif kind == "f2":
                        self.ffn(s, l, 2)
                    elif kind == "xa":
                        self.xattn(s, l)
                    elif l % 2 == 0:
                        self.mixer_even(s, l, t)
                    else:
                        self.mixer_odd(s, l, t)
                    if k + 1 < len(subl):
                        nk, nl = subl[k + 1]
                        self.defer(s, 7, lambda s=s: self.pre_norm_a(s))
                        self.defer(s, 9, lambda s=s: self.pre_norm_b(s))
                        self.defer(s, 11, lambda s=s, nk=nk, nl=nl: self.pre_norm_c(s, prename[nk], nl))
                    else:
                        self.defer(s, 7, lambda s=s, t=t: self.store(s, t))
            self.flush(None)
        S.final_wait(SP, self.out_toks[-2:])

    def store(self, s, t):
        tok = self.S.dma(SP, lambda e: e.dma_start(out=self.yT[:, t * TT:(t + 1) * TT].rearrange("(c p) t -> p c t", p=128), in_=self.H[s][:]),
                         "hout%d" % s, reads=["H%d_%d" % (s, c) for c in range(8)])
        self.out_toks.append(tok)

    def defer(self, s, cd, fn, tag=""):
        self.pending.append([cd, s, fn, tag])

    def flush_post(self):
        last = -1
        for i, it in enumerate(self.pending):
            if it[3] == "post":
                last = i
        for _ in range(last + 1):
            self.pending.pop(0)[2]()

    def tick(self):
        for it in self.pending:
            it[0] -= 1
        while self.pending and self.pending[0][0] <= 0:
            self.pending.pop(0)[2]()

    def flush(self, s):
        last = -1
        for i, it in enumerate(self.pending):
            if s is None or it[1] == s:
                last = i
        for _ in range(last + 1):
            self.pending.pop(0)[2]()

    def rstd_from(self, st, stn, scale, eps, ded=None):
        S = self.S
        if ded is None:
            R, rn = self.rs()
        else:
            R, rn = self.RD[ded], "RD%d" % ded
        S.op(ACT, lambda e: e.activation(out=R[:], in_=st[:], func=AF.Ln, bias=eps, scale=scale), reads=[stn], writes=[rn])
        S.op(ACT, lambda e: e.activation(out=R[:], in_=R[:], func=AF.Exp, scale=-0.5), reads=[rn], writes=[rn])
        return R, rn

    def pre_norm_a(self, s):
        S = self.S
        H = self.H[s]
        for c in range(8):
            if c % 4 != 3:
                S.op(ACT, lambda e, c=c: e.activation(out=self.SQ[:, c, :], in_=H[:, c, :], func=AF.Square),
                     reads=["H%d_%d" % (s, c)], writes=["SQ%d" % c])
            else:
                S.op(POOL, lambda e, c=c: e.tensor_tensor(out=self.SQ[:, c, :], in0=H[:, c, :], in1=H[:, c, :], op=ALU.mult),
                     reads=["H%d_%d" % (s, c)], writes=["SQ%d" % c])

    def pre_norm_b(self, s):
        S = self.S
        st, stn = self.stbank()
        for c in range(8):
            S.op(PE, lambda e, c=c: e.matmul(st[:], lhsT=self.ONESB[:], rhs=self.SQ[:, c, :], start=(c == 0), stop=(c == 7)),
                 reads=["SQ%d" % c, "ONESB"], writes=[stn])
        self.preR[s] = self.rstd_from(st, stn, 1.0 / D, EPS, ded=2 * s + 1)

    def pre_norm_c(self, s, gname, l):
        S = self.S
        H = self.H[s]
        R, rn = self.preR[s]
        for c in range(8):
            S.op(DVE, lambda e, c=c: e.scalar_tensor_tensor(out=self.XN[s][:, c, :], in0=H[:, c, :], scalar=self.vcol(gname, l, c), in1=R[:],
                                                            op0=ALU.mult, op1=ALU.mult),
                 reads=["H%d_%d" % (s, c), rn, "VEC"], writes=["XN%d_%d" % (s, c)])

    def pre_norm(self, s, gname, l):
        self.pre_norm_a(s)
        self.pre_norm_b(s)
        self.pre_norm_c(s, gname, l)

    def out_chunk(self, j, ps, psn, gname, l):
        S = self.S
        if j == 0:
            self.flush(None)
        S.op(ACT, lambda e: e.activation(out=self.F[:, j, :], in_=ps[:], func=AF.Identity, scale=self.vcol(gname, l, j)),
             reads=[psn, "VEC"], writes=["F%d" % j])
        S.op(ACT, lambda e: e.activation(out=self.SQ[:, j, :], in_=ps[:], func=AF.Square), reads=[psn], writes=["SQ%d" % j])

    def out_stat(self, j, st, stn):
        self.S.op(PE, lambda e: e.matmul(st[:], lhsT=self.ONESB[:], rhs=self.SQ[:, j, :], start=(j == 0), stop=(j == 7)),
                  reads=["SQ%d" % j, "ONESB"], writes=[stn])

    def post_a(self, s, prev, st, stn):
        self.out_stat(prev, st, stn)
        self.postR[s] = self.rstd_from(st, stn, 1.0 / D, EPS, ded=2 * s)

    def post_b(self, s, c0, c1):
        S = self.S
        H = self.H[s]
        R, rn = self.postR[s]
        for c in range(c0, c1):
            S.op(POOL, lambda e, c=c: e.tensor_tensor(out=self.F[:, c, :], in0=self.F[:, c, :], in1=R[:], op=ALU.mult),
                 reads=["F%d" % c, rn], writes=["F%d" % c])
            S.op(DVE, lambda e, c=c: e.tensor_tensor(out=H[:, c, :], in0=H[:, c, :], in1=self.F[:, c, :], op=ALU.add),
                 reads=["F%d" % c, "H%d_%d" % (s, c)], writes=["H%d_%d" % (s, c)])

    def defer_post(self, s, prev, st, stn):
        self.defer(s, 1, lambda: self.post_a(s, prev, st, stn), "post")
        self.defer(s, 2, lambda: self.post_b(s, 0, 4), "post")
        self.defer(s, 4, lambda: self.post_b(s, 4, 8), "post")

    def proj_out(self, s, wname, widx, nunits, rhs_list, gname, l):
        S = self.S
        K = len(rhs_list)
        st, stn = self.chainbank(s)
        prev = None
        for u in range(4):
            sl, sn = self.WS.next(S, self, wname, (widx, u), 2 * K * 128)
            for jj in range(2):
                j = 2 * u + jj
                ps, psn = self.bank()
                for k in range(K):
                    rap, rname = rhs_list[k]
                    off = (jj * K + k) * 128
                    S.op(PE, lambda e, ps=ps, sl=sl, off=off, rap=rap, k=k: e.matmul(ps[:], lhsT=self.WB[:, sl, off:off + 128], rhs=rap, start=(k == 0), stop=(k == K - 1)),
                         reads=[sn, rname], writes=[psn])
                self.out_chunk(j, ps, psn, gname, l)
                if prev is not None:
                    self.out_stat(prev, st, stn)
                prev = j
            self.tick()
        self.defer_post(s, prev, st, stn)

    def ffn(self, s, l, which):
        S = self.S
        pre, post = ("f1pre", "f1post") if which == 1 else ("f2pre", "f2post")
        wgu = self.d["wgu%d" % which]
        wd = self.d["wd%d" % which]
        for j in range(22):
            sl, sn = self.WS.next(S, self, "wgu%d" % which, (l, j), 2048)
            g, gn = self.bank()
            u, un = self.bank()
            for k in range(8):
                S.op(PE, lambda e, g=g, sl=sl, k=k: e.matmul(g[:], lhsT=self.WB[:, sl, k * 128:(k + 1) * 128], rhs=self.XN[s][:, k, :], start=(k == 0), stop=(k == 7)),
                     reads=[sn, "XN%d_%d" % (s, k)], writes=[gn])
            for k in range(8):
                S.op(PE, lambda e, u=u, sl=sl, k=k: e.matmul(u[:], lhsT=self.WB[:, sl, 1024 + k * 128:1024 + (k + 1) * 128], rhs=self.XN[s][:, k, :], start=(k == 0), stop=(k == 7)),
                     reads=[sn, "XN%d_%d" % (s, k)], writes=[un])
            sg, sgn = self.sg()
            S.op(ACT, lambda e, g=g, sg=sg: e.activation(out=sg[:], in_=g[:], func=AF.Silu), reads=[gn], writes=[sgn])
            S.op(DVE, lambda e, u=u, sg=sg, j=j: e.tensor_tensor(out=self.HID[:, j, :], in0=sg[:], in1=u[:], op=ALU.mult),
                 reads=[sgn, un], writes=["HID%d" % j])
            self.tick()
        st, stn = self.chainbank(s)
        prev = None
        for j in range(8):
            sl, sn = self.WS.next(S, self, "wd%d" % which, (l, j), 2816)
            ps, psn = self.bank()
            for k in range(22):
                S.op(PE, lambda e, ps=ps, sl=sl, k=k: e.matmul(ps[:], lhsT=self.WB[:, sl, k * 128:(k + 1) * 128], rhs=self.HID[:, k, :], start=(k == 0), stop=(k == 21)),
                     reads=[sn, "HID%d" % k], writes=[psn])
            self.out_chunk(j, ps, psn, post, l)
            if prev is not None:
                self.out_stat(prev, st, stn)
            prev = j
            self.tick()
        self.defer_post(s, prev, st, stn)

    def kv_prologue(self, l):
        S = self.S
        d = self.d
        M = self.SCR
        S.dma(SP, lambda e: e.dma_start(out=M[:, :, 0:NMEM], in_=d["memT"].rearrange("(c p) m -> p c m", p=128)), "memin",
              writes=["SCR%d" % c for c in range(8)])
        st, stn = self.stbank()
        for c in range(8):
            S.op(ACT, lambda e, c=c: e.activation(out=self.SQ[:, c, 0:NMEM], in_=M[:, c, 0:NMEM], func=AF.Square), reads=["SCR%d" % c], writes=["SQ%d" % c])
        for c in range(8):
            S.op(PE, lambda e, c=c: e.matmul(st[:, 0:NMEM], lhsT=self.ONESB[:], rhs=self.SQ[:, c, 0:NMEM], start=(c == 0), stop=(c == 7)),
                 reads=["SQ%d" % c, "ONESB"], writes=[stn])
        R, rn = self.rs()
        S.op(ACT, lambda e: e.activation(out=R[:, 0:NMEM], in_=st[:, 0:NMEM], func=AF.Ln, bias=EPS, scale=1.0 / D), reads=[stn], writes=[rn])
        S.op(ACT, lambda e: e.activation(out=R[:, 0:NMEM], in_=R[:, 0:NMEM], func=AF.Exp, scale=-0.5), reads=[rn], writes=[rn])
        for c in range(8):
            S.op(DVE, lambda e, c=c: e.scalar_tensor_tensor(out=self.HID[:, c, 0:NMEM], in0=M[:, c, 0:NMEM], scalar=self.vcol("xmem", l, c), in1=R[:, 0:NMEM],
                                                            op0=ALU.mult, op1=ALU.mult),
                 reads=["SCR%d" % c, rn, "VEC"], writes=["HID%d" % c])
        for u in range(4):
            sl, sn = self.WS.next(S, self, "wk", (l, u), 2048)
            for jj in range(2):
                j = 2 * u + jj
                ps, psn = self.bank()
                for k in range(8):
                    off = (jj * 8 + k) * 128
                    S.op(PE, lambda e, ps=ps, sl=sl, off=off, k=k: e.matmul(ps[:, 0:NMEM], lhsT=self.WB[:, sl, off:off + 128], rhs=self.HID[:, k, 0:NMEM], start=(k == 0), stop=(k == 7)),
                         reads=[sn, "HID%d" % k], writes=[psn])
                S.op(ACT, lambda e, ps=ps, j=j: e.activation(out=self.KT[:, l, j, :], in_=ps[:, 0:NMEM], func=AF.Identity), reads=[psn], writes=["KT%d" % l])
        for nq in range(4):
            sl, sn = self.WS.next(S, self, "wv", (l, nq), 2048)
            for mc in range(2):
                ps, psn = self.bank()
                for k in range(8):
                    S.op(PE, lambda e, ps=ps, sl=sl, k=k, mc=mc: e.matmul(ps[:, 0:256], lhsT=self.HID[:, k, mc * 128:(mc + 1) * 128], rhs=self.WB[:, sl, k * 256:(k + 1) * 256], start=(k == 0), stop=(k == 7)),
                         reads=[sn, "HID%d" % k], writes=[psn])
                S.op(DVE, lambda e, ps=ps, mc=mc, nq=nq: e.tensor_copy(out=self.V[:, l, mc, nq * 256:(nq + 1) * 256], in_=ps[:, 0:256]), reads=[psn], writes=["V%d" % l])

    def xattn(self, s, l):
        S = self.S
        d = self.d
        Q = self.HID
        for u in range(4):
            sl, sn = self.WS.next(S, self, "wq", (l, u), 2048)
            for jj in range(2):
                j = 2 * u + jj
                ps, psn = self.bank()
                for k in range(8):
                    off = (jj * 8 + k) * 128
                    S.op(PE, lambda e, ps=ps, sl=sl, off=off, k=k: e.matmul(ps[:], lhsT=self.WB[:, sl, off:off + 128], rhs=self.XN[s][:, k, :], start=(k == 0), stop=(k == 7)),
                         reads=[sn, "XN%d_%d" % (s, k)], writes=[psn])
                S.op(ACT, lambda e, ps=ps, j=j: e.activation(out=Q[:, j, :], in_=ps[:], func=AF.Identity, scale=1.0 / 16.0), reads=[psn], writes=["HID%d" % j])
            self.tick()
        def scores(h):
            eb = h % 2
            for mc in range(2):
                ps, psn = self.bank()
                for dd in range(2):
                    dc = 2 * h + dd
                    S.op(PE, lambda e, ps=ps, dc=dc, mc=mc, dd=dd: e.matmul(ps[:], lhsT=self.KT[:, l, dc, mc * 128:(mc + 1) * 128], rhs=Q[:, dc, :], start=(dd == 0), stop=(dd == 1)),
                         reads=["KT%d" % l, "HID%d" % dc], writes=[psn])
                S.op(ACT, lambda e, ps=ps, mc=mc, eb=eb: e.activation(out=self.E[:, eb, mc, :], in_=ps[:], func=AF.Exp), reads=[psn], writes=["E%d_%d" % (eb, mc)])

        def finish(h):
            eb = h % 2
            sums, sumn = self.stbank()
            for mc in range(2):
                S.op(PE, lambda e, mc=mc, eb=eb, sums=sums: e.matmul(sums[:], lhsT=self.ONESB[:], rhs=self.E[:, eb, mc, :], start=(mc == 0), stop=(mc == 1)),
                     reads=["E%d_%d" % (eb, mc), "ONESB"], writes=[sumn])
            R, rn = self.rs()
            S.op(ACT, lambda e, R=R, sums=sums: e.activation(out=R[:], in_=sums[:], func=AF.Ln), reads=[sumn], writes=[rn])
            S.op(ACT, lambda e, R=R: e.activation(out=R[:], in_=R[:], func=AF.Exp, scale=-1.0), reads=[rn], writes=[rn])
            for dd in range(2):
                dc = 2 * h + dd
                ps, psn = self.bank()
                for mc in range(2):
                    S.op(PE, lambda e, ps=ps, mc=mc, dc=dc, eb=eb: e.matmul(ps[:], lhsT=self.V[:, l, mc, dc * 128:(dc + 1) * 128], rhs=self.E[:, eb, mc, :], start=(mc == 0), stop=(mc == 1)),
                         reads=["V%d" % l, "E%d_%d" % (eb, mc)], writes=[psn])
                S.op(DVE, lambda e, ps=ps, dc=dc, R=R: e.tensor_tensor(out=Q[:, 8 + dc, :], in0=ps[:], in1=R[:], op=ALU.mult), reads=[psn, rn], writes=["HID%d" % (8 + dc)])

        scores(0)
        for h in range(4):
            if h + 1 < 4:
                scores(h + 1)
            finish(h)
            self.tick()
        self.proj_out(s, "wo", l, 4, [(Q[:, 8 + k, :], "HID%d" % (8 + k)) for k in range(8)], "xpost", l)

    def ln_bcast(self, src_chunks, nch):
        S = self.S
        s1, s1n = self.stbank()
        s2, s2n = self.stbank()
        n = float(nch * 128)
        for c, (ap, rn_) in enumerate(src_chunks):
            S.op(PE, lambda e, c=c, ap=ap: e.matmul(s1[:], lhsT=self.ONESF[:], rhs=ap, start=(c == 0), stop=(c == nch - 1)), reads=[rn_, "ONESF"], writes=[s1n])
        for c, (ap, rn_) in enumerate(src_chunks):
            sg, sgn = self.sg()
            S.op(ACT, lambda e, ap=ap, sg=sg: e.activation(out=sg[:], in_=ap, func=AF.Square), reads=[rn_], writes=[sgn])
            S.op(PE, lambda e, c=c, sg=sg: e.matmul(s2[:], lhsT=self.ONESF[:], rhs=sg[:], start=(c == 0), stop=(c == nch - 1)), reads=[sgn, "ONESF"], writes=[s2n])
        MEAN, mn = self.rs()
        VAR, vn = self.rs()
        S.op(DVE, lambda e: e.tensor_scalar(out=MEAN[:], in0=s1[:], scalar1=1.0 / n, scalar2=None, op0=ALU.mult), reads=[s1n], writes=[mn])
        S.op(DVE, lambda e: e.tensor_tensor(out=VAR[:], in0=MEAN[:], in1=MEAN[:], op=ALU.mult), reads=[mn], writes=[vn])
        S.op(DVE, lambda e: e.scalar_tensor_tensor(out=VAR[:], in0=s2[:], scalar=1.0 / n, in1=VAR[:], op0=ALU.mult, op1=ALU.subtract), reads=[s2n, vn], writes=[vn])
        S.op(ACT, lambda e: e.activation(out=VAR[:], in_=VAR[:], func=AF.Ln, bias=EPS, scale=1.0), reads=[vn], writes=[vn])
        S.op(ACT, lambda e: e.activation(out=VAR[:], in_=VAR[:], func=AF.Exp, scale=-0.5), reads=[vn], writes=[vn])
        S.op(DVE, lambda e: e.tensor_tensor(out=MEAN[:], in0=MEAN[:], in1=VAR[:], op=ALU.mult), reads=[mn, vn], writes=[mn])
        return VAR, vn, MEAN, mn

    def mixer_even(self, s, l, t):
        S = self.S
        d = self.d
        ev = l // 2
        Z = self.SCR
        ZB = self.ZB
        CA = self.F
        Y = self.HID
        for c in range(4):
            S.op(POOL, lambda e, c=c: e.tensor_copy(out=ZB[:, c, 0:30], in_=self.CARZ[:, ev, c, :]), reads=["CARZ%d" % ev], writes=["ZB%d" % c])
        for c in range(4):
            sl, sn = self.WS.next(S, self, "evin", (ev, c), 2048)
            g, gn = self.bank()
            v, vn = self.bank()
            for k in range(8):
                S.op(PE, lambda e, g=g, sl=sl, k=k: e.matmul(g[:], lhsT=self.WB[:, sl, k * 128:(k + 1) * 128], rhs=self.XN[s][:, k, :], start=(k == 0), stop=(k == 7)),
                     reads=[sn, "XN%d_%d" % (s, k)], writes=[gn])
            for k in range(8):
                S.op(PE, lambda e, v=v, sl=sl, k=k: e.matmul(v[:], lhsT=self.WB[:, sl, 1024 + k * 128:1024 + (k + 1) * 128], rhs=self.XN[s][:, k, :], start=(k == 0), stop=(k == 7)),
                     reads=[sn, "XN%d_%d" % (s, k)], writes=[vn])
            sg, sgn = self.sg()
            S.op(ACT, lambda e, g=g, sg=sg: e.activation(out=sg[:], in_=g[:], func=AF.Tanh, scale=0.5), reads=[gn], writes=[sgn])
            S.op(DVE, lambda e, v=v, sg=sg, c=c: e.scalar_tensor_tensor(out=ZB[:, c, 30:30 + TT], in0=sg[:], scalar=1.0, in1=v[:], op0=ALU.add, op1=ALU.mult),
                 reads=[sgn, vn], writes=["ZB%d" % c])
            self.tick()
        self.flush_post()
        for uu in range(4):
            sl, sn = self.WS.next(S, self, "evin", (ev, 4 + uu), 2048)
            for jj in range(2):
                cc = (uu % 2) * 2 + jj
                ps, psn = self.bank()
                for k in range(8):
                    off = (jj * 8 + k) * 128
                    S.op(PE, lambda e, ps=ps, sl=sl, off=off, k=k: e.matmul(ps[:], lhsT=self.WB[:, sl, off:off + 128], rhs=self.XN[s][:, k, :], start=(k == 0), stop=(k == 7)),
                         reads=[sn, "XN%d_%d" % (s, k)], writes=[psn])
                if uu < 2:
                    S.op(ACT, lambda e, ps=ps, cc=cc: e.activation(out=CA[:, 4 + cc, :], in_=ps[:], func=AF.Gelu), reads=[psn], writes=["F%d" % (4 + cc)])
                else:
                    S.op(ACT, lambda e, ps=ps, cc=cc: e.activation(out=Z[:, 4 + cc, 0:TT], in_=ps[:], func=AF.Gelu), reads=[psn], writes=["SCR%d" % (4 + cc)])
            self.tick()
        RSTD, rn, MR, mrn = self.ln_bcast([(Z[:, 4 + c, 0:TT], "SCR%d" % (4 + c)) for c in range(4)], 4)
        for c in range(4):
            S.op(DVE, lambda e, c=c: e.tensor_tensor(out=Z[:, 4 + c, 0:TT], in0=Z[:, 4 + c, 0:TT], in1=RSTD[:], op=ALU.mult), reads=["SCR%d" % (4 + c), rn], writes=["SCR%d" % (4 + c)])
            S.op(DVE, lambda e, c=c: e.tensor_tensor(out=Z[:, 4 + c, 0:TT], in0=Z[:, 4 + c, 0:TT], in1=MR[:], op=ALU.subtract), reads=["SCR%d" % (4 + c), mrn], writes=["SCR%d" % (4 + c)])
            S.op(DVE, lambda e, c=c: e.tensor_scalar(out=Y[:, 8 + c, :], in0=Z[:, 4 + c, 0:TT], scalar1=self.vcol("slg", ev, c), scalar2=self.vcol("slb", ev, c), op0=ALU.mult, op1=ALU.add),
                 reads=["SCR%d" % (4 + c), "VEC"], writes=["HID%d" % (8 + c)])
        for c in range(4):
            ps, psn = self.bank()
            for hf in range(2):
                sl, sn = self.WS.next(S, self, "evcv", (ev, 2 * c + hf), 2048)
                for jl in range(16 if hf == 0 else 15):
                    j = hf * 16 + jl
                    S.op(PE, lambda e, ps=ps, sl=sl, jl=jl, j=j, c=c: e.matmul(ps[:], lhsT=self.WB[:, sl, jl * 128:(jl + 1) * 128], rhs=ZB[:, c, j:j + TT], start=(j == 0), stop=(j == CW - 1)),
                         reads=[sn, "ZB%d" % c], writes=[psn])
            S.op(ACT, lambda e, ps=ps, c=c: e.activation(out=CA[:, c, :], in_=ps[:], func=AF.Identity, scale=0.5, bias=self.vcol("cb", ev, c)), reads=[psn, "VEC"], writes=["F%d" % c])
            self.tick()
        S.op(POOL, lambda e: e.tensor_copy(out=self.CARZ[:, ev, :, :], in_=ZB[:, 0:4, TT:TT + 30]), reads=["ZB%d" % c for c in range(4)], writes=["CARZ%d" % ev])
        RSTD2, rn2, MR2, mrn2 = self.ln_bcast([(CA[:, c, :], "F%d" % c) for c in range(4)], 4)
        for c in range(4):
            S.op(DVE, lambda e, c=c: e.tensor_tensor(out=CA[:, c, :], in0=CA[:, c, :], in1=RSTD2[:], op=ALU.mult), reads=["F%d" % c, rn2], writes=["F%d" % c])
            S.op(DVE, lambda e, c=c: e.tensor_tensor(out=CA[:, c, :], in0=CA[:, c, :], in1=MR2[:], op=ALU.subtract), reads=["F%d" % c, mrn2], writes=["F%d" % c])
            S.op(ACT, lambda e, c=c: e.activation(out=Y[:, c, :], in_=CA[:, c, :], func=AF.Silu, scale=self.vcol("clg", ev, c), bias=self.vcol("clb", ev, c)),
                 reads=["F%d" % c, "VEC"], writes=["HID%d" % c])
        for tc in range(4):
            for h in range(4):
                S.op(PE, lambda e, tc=tc, h=h: e.transpose(self.TPB[:, h * 128:(h + 1) * 128], Y[:, 8 + h, tc * 128:(tc + 1) * 128], self.IDB[:]),
                     reads=["HID%d" % (8 + h), "IDB"], writes=["TPB"])
            S.op(ACT, lambda e, tc=tc: e.activation(out=Y[:, 12 + tc, :], in_=self.TPB[:, 0:512], func=AF.Identity), reads=["TPB"], writes=["HID%d" % (12 + tc)])
        for h in range(4):
            ps, psn = self.bank()
            for tc in range(4):
                S.op(PE, lambda e, ps=ps, tc=tc, h=h: e.matmul(ps[:, tc * 128:(tc + 1) * 128], lhsT=Y[:, 12 + tc, h * 128:(h + 1) * 128], rhs=self.WST[:, ev, h * 128:(h + 1) * 128], start=True, stop=True),
                     reads=["HID%d" % (12 + tc), "WST"], writes=[psn])
            sg, sgn = self.sg()
            for tc in range(4):
                S.op(DVE, lambda e, ps=ps, sg=sg, tc=tc, h=h: e.tensor_tensor(out=sg[:, tc * 128:(tc + 1) * 128], in0=ps[:, tc * 128:(tc + 1) * 128], in1=self.BSB[:, ev, h * 128:(h + 1) * 128], op=ALU.add),
                     reads=[psn, "BSB"], writes=[sgn])
            S.op(DVE, lambda e, sg=sg, h=h: e.tensor_tensor(out=Y[:, 4 + h, :], in0=sg[:], in1=CA[:, 4 + h, :], op=ALU.mult), reads=[sgn, "F%d" % (4 + h)], writes=["HID%d" % (4 + h)])
            self.tick()
        self.proj_out(s, "evout", ev, 4, [(Y[:, k, :], "HID%d" % k) for k in range(8)], "mpost", l)

    def mixer_odd(self, s, l, t):
        S = self.S
        d = self.d
        od = l // 2
        PP = self.SCR
        Dm = self.HID
        for c in range(8):
            S.op(POOL, lambda e, c=c: e.tensor_copy(out=PP[:, c, 0:16], in_=self.CARP[:, od, c, :]), reads=["CARP%d" % od], writes=["SCR%d" % c])
        def cascade(c):
            g = c // 2
            w = 2 << g
            eng = DVE if c % 2 == 0 else POOL
            ta, tb = (0, 1) if c % 2 == 0 else (2, 3)
            tan, tbn = ["F%d" % (2 * ta), "F%d" % (2 * ta + 1)], ["F%d" % (2 * tb), "F%d" % (2 * tb + 1)]
            TA = self.F[:, 2 * ta:2 * ta + 2, :].rearrange("p a b -> p (a b)")
            TB = self.F[:, 2 * tb:2 * tb + 2, :].rearrange("p a b -> p (a b)")
            src, srcn = PP[:, c, :], "SCR%d" % c
            sh = 1
            cur, curn = None, None
            for step in range(g + 1):
                dst, dstn = (TA, tan) if step % 2 == 0 else (TB, tbn)
                lo = 2 * sh - 1
                inp, inpn = (src, [srcn]) if step == 0 else (cur, curn)
                S.op(eng, lambda e, dst=dst, inp=inp, lo=lo, sh=sh: e.tensor_tensor(out=dst[:, lo:528], in0=inp[:, lo:528], in1=inp[:, lo - sh:528 - sh], op=ALU.add),
                     reads=inpn, writes=dstn)
                cur, curn = dst, dstn
                sh *= 2
            S.op(DVE, lambda e, cur=cur, c=c, w=w: e.scalar_tensor_tensor(out=Dm[:, c, :], in0=cur[:, 16:528], scalar=1.0 / w, in1=PP[:, c, 16:528], op0=ALU.mult, op1=ALU.subtract),
                 reads=curn + ["SCR%d" % c], writes=["HID%d" % c])
            if t == 0:
                sg, sgn = self.sg()
                S.op(eng, lambda e, cur=cur, sg=sg, g=g: e.tensor_tensor(out=sg[:, 0:16], in0=cur[:, 16:32], in1=self.CST[:, C_INV + g * 16:C_INV + (g + 1) * 16], op=ALU.mult),
                     reads=curn + ["CST"], writes=[sgn])
                S.op(eng, lambda e, sg=sg, c=c: e.tensor_tensor(out=Dm[:, c, 0:16], in0=sg[:, 0:16], in1=PP[:, c, 16:32], op=ALU.subtract),
                     reads=[sgn, "SCR%d" % c, "HID%d" % c], writes=["HID%d" % c])
        self.flush_post()
        for u in (3, 2, 1, 0):
            sl, sn = self.WS.next(S, self, "odin", (od, u), 2048)
            for jj in range(2):
                j = 2 * u + jj
                ps, psn = self.bank()
                for k in range(8):
                    off = (jj * 8 + k) * 128
                    S.op(PE, lambda e, ps=ps, sl=sl, off=off, k=k: e.matmul(ps[:], lhsT=self.WB[:, sl, off:off + 128], rhs=self.XN[s][:, k, :], start=(k == 0), stop=(k == 7)),
                         reads=[sn, "XN%d_%d" % (s, k)], writes=[psn])
                S.op(ACT, lambda e, ps=ps, j=j: e.activation(out=PP[:, j, 16:16 + TT], in_=ps[:], func=AF.Identity), reads=[psn], writes=["SCR%d" % j])
            cascade(2 * u)
            cascade(2 * u + 1)
            self.tick()
        S.op(POOL, lambda e: e.tensor_copy(out=self.CARP[:, od, :, :], in_=PP[:, :, TT:TT + 16]), reads=["SCR%d" % c for c in range(8)], writes=["CARP%d" % od])
        sl, sn = self.WS.next(S, self, "odg", (od,), 2048)
        for g in (3, 2, 1, 0):
            for jn in range(2):
                ps, psn = self.bank()
                for kc in range(2):
                    off = g * 512 + kc * 256 + jn * 128
                    S.op(PE, lambda e, ps=ps, sl=sl, off=off, kc=kc, g=g: e.matmul(ps[:], lhsT=self.WB[:, sl, off:off + 128], rhs=Dm[:, 2 * g + kc, :], start=(kc == 0), stop=(kc == 1)),
                         reads=[sn, "HID%d" % (2 * g + kc)], writes=[psn])
                cc = 2 * g + jn
                S.op(ACT, lambda e, ps=ps, cc=cc: e.activation(out=Dm[:, 8 + cc, :], in_=ps[:], func=AF.Identity, scale=self.vcol("osc", od, cc)), reads=[psn, "VEC"], writes=["HID%d" % (8 + cc)])
            self.tick()
        self.proj_out(s, "odout", od, 4, [(Dm[:, 8 + k, :], "HID%d" % (8 + k)) for k in range(8)], "mpost", l)


def build_program(ntiles=8, layers=(0, 1, 2, 3), debug=False):
    nc = bass.Bass("TRN2", target_bir_lowering=False)
    B = Builder(nc, ntiles, list(layers), debug)
    st = contextlib.ExitStack()
    with st:
        B.alloc(st)
        ws0 = WStream(None)
        B.emit_all(Sched(), ws0)
        B.out_toks = []
        S = Sched()
        B.emit_all(S, WStream(ws0.rec))
        S.emit(nc, st)
    return nc


def blk2(W, K):
    N = W.shape[1]
    a = W.reshape(K, 128, N // 256, 2, 128)
    return np.ascontiguousarray(a.transpose(2, 1, 3, 0, 4)).reshape(N // 256, 128, 2 * K * 128)


def colvec(v):
    n = v.shape[0] // 128
    return np.ascontiguousarray(v.reshape(n, 128).T)


def host_weights(inp):
    f = lambda a: np.asarray(a, dtype=np.float32)
    w = {}
    for which in (1, 2):
        gu = f({1: inp["ffn1_w_gu"], 2: inp["ffn2_w_gu"]}[which])
        g = gu[:, :, :DFF].reshape(4, 8, 128, 22, 128)
        u = gu[:, :, DFF:].reshape(4, 8, 128, 22, 128)
        a = np.stack([g, u], axis=0)
        w["wgu%d" % which] = np.ascontiguousarray(a.transpose(1, 4, 3, 0, 2, 5)).reshape(4, 22, 128, 2048)
        dn = f({1: inp["ffn1_w_down"], 2: inp["ffn2_w_down"]}[which]).reshape(4, 22, 128, 8, 128)
        w["wd%d" % which] = np.ascontiguousarray(dn.transpose(0, 3, 2, 1, 4)).reshape(4, 8, 128, 2816)
    evin = f(inp["ev_w_in"])
    units = []
    for e in range(2):
        W = evin[e].reshape(8, 128, 16, 128)
        us = []
        for c in range(4):
            us.append(np.stack([W[:, :, 4 + c, :], W[:, :, c, :]], axis=0))
        for b0 in (8, 10, 12, 14):
            us.append(np.stack([W[:, :, b0, :], W[:, :, b0 + 1, :]], axis=0))
        units.append(np.stack([np.ascontiguousarray(x.transpose(2, 0, 1, 3)).reshape(128, 2048) for x in us], axis=0))
    w["evin"] = np.stack(units, axis=0)
    w["evout"] = np.stack([blk2(f(inp["ev_w_out"])[e], 8) for e in range(2)], axis=0)
    cwf = f(inp["ev_conv_w"])
    evcv = np.zeros((2, 8, 128, 16, 128), np.float32)
    pidx = np.arange(128)
    for e in range(2):
        for c in range(4):
            for j in range(CW):
                evcv[e, 2 * c + j // 16, pidx, j % 16, pidx] = cwf[e, j, c * 128:(c + 1) * 128]
    w["evcv"] = evcv.reshape(2, 8, 128, 2048)
    w["odin"] = np.stack([blk2(f(inp["od_w_in"])[o], 8) for o in range(2)], axis=0)
    w["odout"] = np.stack([blk2(f(inp["od_w_out"])[o], 8) for o in range(2)], axis=0)
    og = f(inp["od_w_group"]).reshape(2, 4, 2, 128, 256)
    w["odg"] = np.ascontiguousarray(og.transpose(0, 3, 1, 2, 4)).reshape(2, 128, 2048)
    w["wq"] = np.stack([blk2(f(inp["xa_w_q"])[l], 8) for l in range(4)], axis=0)
    w["wo"] = np.stack([blk2(f(inp["xa_w_o"])[l], 8) for l in range(4)], axis=0)
    kv = f(inp["xa_w_kv"])
    w["wk"] = np.stack([blk2(kv[l][:, :D], 8) for l in range(4)], axis=0)
    wv = kv[:, :, D:].reshape(4, 8, 128, 4, 256)
    w["wv"] = np.ascontiguousarray(wv.transpose(0, 3, 2, 1, 4)).reshape(4, 4, 128, 2048)
    vec = np.zeros((128, NVEC), np.float32)
    names = {"f1pre": "ffn1_pre_g", "f1post": "ffn1_post_g", "mpre": "mix_pre_g", "mpost": "mix_post_g", "xpre": "xa_pre_g",
             "xmem": "xa_mem_g", "xpost": "xa_post_g", "f2pre": "ffn2_pre_g", "f2post": "ffn2_post_g"}
    for l in range(4):
        for k, nm in names.items():
            o = VCOLS[(k, l)]
            vec[:, o:o + 8] = colvec(f(inp[nm])[l])
    en = {"cb": "ev_conv_b", "clg": "ev_conv_ln_g", "clb": "ev_conv_ln_b", "slg": "ev_sgu_ln_g", "slb": "ev_sgu_ln_b"}
    for e in range(2):
        for k, nm in en.items():
            o = VCOLS[(k, e)]
            vec[:, o:o + 4] = colvec(f(inp[nm])[e])
        cw = f(inp["ev_conv_w"])[e]
        o = VCOLS[("cw", e)]
        vec[:, o:o + 4 * CW] = np.ascontiguousarray(cw.T.reshape(4, 128, CW).transpose(1, 0, 2)).reshape(128, 4 * CW)
    for o_ in range(2):
        o = VCOLS[("osc", o_)]
        vec[:, o:o + 8] = colvec(f(inp["od_scale"])[o_])
    w["vec"] = vec
    cst = np.zeros((128, NCST), np.float32)
    s = np.arange(128)
    cst[:, C_MASK:C_MASK + 128] = (s[:, None] <= s[None, :]).astype(np.float32)
    cst[:, C_ID:C_ID + 128] = np.eye(128, dtype=np.float32)
    for g, wd in enumerate((2, 4, 8, 16)):
        cst[:, C_INV + g * 16:C_INV + (g + 1) * 16] = 1.0 / np.minimum(np.arange(1, 17), wd)[None, :]
    w["cst"] = cst
    sw = f(inp["ev_sgu_w"])
    w["wst"] = np.ascontiguousarray(sw.transpose(0, 3, 1, 2)).reshape(2, 128, 512)
    sb_ = f(inp["ev_sgu_b"]).reshape(2, 1, 512)
    w["bsb"] = np.ascontiguousarray(np.broadcast_to(sb_, (2, 128, 512)))
    return w


_PROG = {}


def run(inputs, ntiles=8, layers=(0, 1, 2, 3), batches=None, x_override=None, debug=False):
    key = (ntiles, tuple(layers), debug)
    if key not in _PROG:
        _PROG[key] = build_program(ntiles, layers, debug)
    nc = _PROG[key]
    w = host_weights(inputs)
    x = np.asarray(inputs["x"] if x_override is None else x_override, dtype=np.float32)
    mem = np.asarray(inputs["mem"], dtype=np.float32)
    nb = x.shape[0] if batches is None else batches
    in_maps = []
    for b in range(nb):
        m = dict(w)
        m["xT"] = np.ascontiguousarray(x[b, :ntiles * TT, :].T)
        m["memT"] = np.ascontiguousarray(mem[b].T)
        in_maps.append(m)
    res = run_bass_kernel_spmd(nc, in_maps, core_ids=list(range(nb)))
    out = np.stack([np.ascontiguousarray(r["yT"].T) for r in res.results], axis=0)
    return out


def kernel(**inputs):
    return run(inputs).astype(np.float32)
```
